# Optimizing a Trainium2 kernel written in Bass

```python
import jax, jax.numpy as jnp
from jax import lax
import numpy as np

D_MODEL = 1024
BATCH = 8
SEQ = 4096
DEPTH = 4

N_RET_HEADS = 4
RET_HEAD_DIM = 128
RET_WIDTH = N_RET_HEADS * RET_HEAD_DIM
POOL_WINDOWS = (2, 4, 8, 16)
POOL_GROUP_DIM = 128
POOL_WIDTH = len(POOL_WINDOWS) * POOL_GROUP_DIM
MIX_WIDTH = RET_WIDTH + POOL_WIDTH
IN_WIDTH = 4 * RET_WIDTH + POOL_WIDTH
CHUNK = 128
ROPE_BASE = 10000.0
CONV_WIDTH = 31
D_FF = -(-8 * D_MODEL // (3 * 256)) * 256
EPS = 1e-6
N_EVEN = (DEPTH + 1) // 2
N_ODD = DEPTH // 2

kernel_name = "retention_pool_conformer_hybrid"


def rms_norm(x, g):
    x32 = x.astype(jnp.float32)
    y = x32 * lax.rsqrt(jnp.mean(x32 * x32, axis=-1, keepdims=True) + EPS)
    return (y * g).astype(x.dtype)


def rope(t, positions):
    half = t.shape[-1] // 2
    inv_freq = ROPE_BASE ** (-jnp.arange(half, dtype=jnp.float32) / half)
    ang = positions.astype(jnp.float32)[..., None] * inv_freq
    cos = jnp.cos(ang)[:, :, None, :]
    sin = jnp.sin(ang)[:, :, None, :]
    t1, t2 = t[..., :half], t[..., half:]
    return jnp.concatenate([t1 * cos - t2 * sin, t1 * sin + t2 * cos], axis=-1)


def retention(q, k, v, positions):
    B, S, H, Dh = q.shape
    n_chunks = S // CHUNK
    q = rope(q, positions)
    k = rope(k, positions) * (Dh ** -0.5)
    log_gamma = jnp.log1p(-(2.0 ** (-5.0 - jnp.arange(H, dtype=jnp.float32))))
    idx = jnp.arange(CHUNK, dtype=jnp.float32)
    rel = idx[:, None] - idx[None, :]
    decay_mask = jnp.where(rel >= 0, jnp.exp(log_gamma[:, None, None] * jnp.maximum(rel, 0.0)), 0.0)
    q_decay = jnp.exp(log_gamma[:, None] * (idx + 1.0))
    k_decay = jnp.exp(log_gamma[:, None] * (CHUNK - 1.0 - idx))
    chunk_decay = jnp.exp(log_gamma * CHUNK)

    def to_chunks(t):
        return t.reshape(B, n_chunks, CHUNK, H, Dh).transpose(0, 3, 1, 2, 4)

    qc, kc, vc = to_chunks(q), to_chunks(k), to_chunks(v)
    scores = jnp.einsum('bhncd,bhnmd->bhncm', qc, kc) * decay_mask[:, None]
    o_intra = jnp.einsum('bhncm,bhnme->bhnce', scores, vc)
    kv = jnp.einsum('bhnmd,bhnme->nbhde', kc * k_decay[:, None, :, None], vc)

    def step(state, kv_n):
        return state * chunk_decay[None, :, None, None] + kv_n, state

    _, prev_states = lax.scan(step, jnp.zeros((B, H, Dh, Dh), jnp.float32), kv)
    o_cross = jnp.einsum('bhncd,nbhde->bhnce', qc * q_decay[:, None, :, None], prev_states)
    return (o_intra + o_cross).transpose(0, 2, 3, 1, 4).reshape(B, S, H, Dh)


def multiscale_pool(u, pool_w, pool_scale):
    B, S, _ = u.shape
    u32 = u.astype(jnp.float32)
    cs0 = jnp.pad(jnp.cumsum(u32, axis=1), ((0, 0), (1, 0), (0, 0)))
    t = jnp.arange(S)
    outs = []
    for gi, w in enumerate(POOL_WINDOWS):
        c = cs0[:, :, gi * POOL_GROUP_DIM:(gi + 1) * POOL_GROUP_DIM]
        lower = jnp.pad(c[:, :S + 1 - w], ((0, 0), (w - 1, 0), (0, 0)))
        count = jnp.minimum(t + 1, w).astype(jnp.float32)[None, :, None]
        y = (c[:, 1:] - lower) / count - u32[:, :, gi * POOL_GROUP_DIM:(gi + 1) * POOL_GROUP_DIM]
        outs.append(jnp.einsum('bsc,cd->bsd', y.astype(u.dtype), pool_w[gi]))
    return jnp.concatenate(outs, axis=-1) * pool_scale


def retention_pool_mixer(h, positions, w_in, ret_norm_g, pool_w, pool_scale, w_out):
    B, S, _ = h.shape
    proj = h @ w_in
    q, k, v, g, u = jnp.split(proj, [RET_WIDTH, 2 * RET_WIDTH, 3 * RET_WIDTH, 4 * RET_WIDTH], axis=-1)

    def heads(t):
        return t.reshape(B, S, N_RET_HEADS, RET_HEAD_DIM).astype(jnp.float32)

    o = retention(heads(q), heads(k), heads(v), positions)
    mu = jnp.mean(o, axis=-1, keepdims=True)
    var = jnp.mean(jnp.square(o - mu), axis=-1, keepdims=True)
    o = ((o - mu) * lax.rsqrt(var + EPS)).reshape(B, S, RET_WIDTH) * ret_norm_g
    ret_out = (jax.nn.silu(g.astype(jnp.float32)) * o).astype(h.dtype)
    pool_out = multiscale_pool(u, pool_w, pool_scale).astype(h.dtype)
    return jnp.concatenate([ret_out, pool_out], axis=-1) @ w_out


def conformer_conv(h, w_pw1, b_pw1, w_dw, b_dw, ln_g, ln_b, w_pw2, b_pw2):
    a, gate = jnp.split(h @ w_pw1 + b_pw1, 2, axis=-1)
    u = a * jax.nn.sigmoid(gate)
    dw = lax.conv_general_dilated(
        u, w_dw[:, None, :], window_strides=(1,), padding=((CONV_WIDTH - 1, 0),),
        dimension_numbers=('NWC', 'WIO', 'NWC'), feature_group_count=D_MODEL) + b_dw
    d32 = dw.astype(jnp.float32)
    mu = jnp.mean(d32, axis=-1, keepdims=True)
    var = jnp.mean(jnp.square(d32 - mu), axis=-1, keepdims=True)
    z = (d32 - mu) * lax.rsqrt(var + EPS) * ln_g + ln_b
    z = jax.nn.silu(z).astype(h.dtype)
    return z @ w_pw2 + b_pw2


def swiglu(h, w_gate, w_up, w_down):
    return (jax.nn.silu(h @ w_gate) * (h @ w_up)) @ w_down


def setup_inputs(seed: int = 0) -> dict:
    key = jax.random.key(seed)
    ks = jax.random.split(key, 24)
    f32 = jnp.float32

    def dense(k, shape, fan_in):
        return jax.random.normal(k, shape, f32) * (fan_in ** -0.5)

    def gain(k, shape):
        return 1.0 + 0.05 * jax.random.normal(k, shape, f32)

    def bias(k, shape):
        return 0.01 * jax.random.normal(k, shape, f32)

    return {
        "x": jax.random.normal(ks[0], (BATCH, SEQ, D_MODEL), f32),
        "positions": jnp.broadcast_to(jnp.arange(SEQ, dtype=jnp.int32), (BATCH, SEQ)),
        "mixer_norm_g": gain(ks[1], (DEPTH, D_MODEL)),
        "ffn_norm_g": gain(ks[2], (DEPTH, D_MODEL)),
        "final_norm_g": gain(ks[3], (D_MODEL,)),
        "ret_w_in": dense(ks[4], (N_EVEN, D_MODEL, IN_WIDTH), D_MODEL),
        "ret_norm_g": gain(ks[5], (N_EVEN, RET_WIDTH)),
        "pool_w": dense(ks[6], (N_EVEN, len(POOL_WINDOWS), POOL_GROUP_DIM, POOL_GROUP_DIM), POOL_GROUP_DIM),
        "pool_scale": gain(ks[7], (N_EVEN, POOL_WIDTH)),
        "mix_w_out": dense(ks[8], (N_EVEN, MIX_WIDTH, D_MODEL), MIX_WIDTH),
        "conv_w_pw1": dense(ks[9], (N_ODD, D_MODEL, 2 * D_MODEL), D_MODEL),
        "conv_b_pw1": bias(ks[10], (N_ODD, 2 * D_MODEL)),
        "conv_w_dw": dense(ks[11], (N_ODD, CONV_WIDTH, D_MODEL), CONV_WIDTH),
        "conv_b_dw": bias(ks[12], (N_ODD, D_MODEL)),
        "conv_ln_g": gain(ks[13], (N_ODD, D_MODEL)),
        "conv_ln_b": bias(ks[14], (N_ODD, D_MODEL)),
        "conv_w_pw2": dense(ks[15], (N_ODD, D_MODEL, D_MODEL), D_MODEL),
        "conv_b_pw2": bias(ks[16], (N_ODD, D_MODEL)),
        "ffn_w_gate": dense(ks[17], (DEPTH, D_MODEL, D_FF), D_MODEL),
        "ffn_w_up": dense(ks[18], (DEPTH, D_MODEL, D_FF), D_MODEL),
        "ffn_w_down": dense(ks[19], (DEPTH, D_FF, D_MODEL), D_FF),
    }


def reference(x, positions, mixer_norm_g, ffn_norm_g, final_norm_g, ret_w_in, ret_norm_g,
              pool_w, pool_scale, mix_w_out, conv_w_pw1, conv_b_pw1, conv_w_dw, conv_b_dw,
              conv_ln_g, conv_ln_b, conv_w_pw2, conv_b_pw2, ffn_w_gate, ffn_w_up, ffn_w_down):
    h = x
    for layer in range(DEPTH):
        i = layer // 2
        hn = rms_norm(h, mixer_norm_g[layer])
        if layer % 2 == 0:
            h = h + retention_pool_mixer(hn, positions, ret_w_in[i], ret_norm_g[i],
                                         pool_w[i], pool_scale[i], mix_w_out[i])
        else:
            h = h + conformer_conv(hn, conv_w_pw1[i], conv_b_pw1[i], conv_w_dw[i], conv_b_dw[i],
                                   conv_ln_g[i], conv_ln_b[i], conv_w_pw2[i], conv_b_pw2[i])
        h = h + swiglu(rms_norm(h, ffn_norm_g[layer]), ffn_w_gate[layer], ffn_w_up[layer], ffn_w_down[layer])
    return rms_norm(h, final_norm_g)
```

```python
import numpy as np
from contextlib import ExitStack
import concourse.bass as bass
import concourse.mybir as mybir
from concourse.bass_utils import run_bass_kernel_spmd

F32 = mybir.dt.float32
BF16 = mybir.dt.bfloat16
I32 = mybir.dt.int32
ALU = mybir.AluOpType
AF = mybir.ActivationFunctionType

D = 1024
DFF = 2816
NJ = DFF // 128
SEQ = 4096
DEPTH = 4
T = 1024
EPS = 1e-6
PI = float(np.pi)
EPOCH = 12000


class Buf:
    __slots__ = ("name", "lw", "rd", "const")

    def __init__(self, name, const=False):
        self.name = name
        self.lw = None
        self.rd = []
        self.const = const


class DmaSem:
    __slots__ = ("name", "count", "handle")

    def __init__(self, name):
        self.name = name
        self.count = 0
        self.handle = None


class Op:
    __slots__ = ("eng", "idx", "fn", "waits", "signaled", "dsem", "dval", "sig_epoch", "sig_val")

    def __init__(self, eng, idx, fn):
        self.eng = eng
        self.idx = idx
        self.fn = fn
        self.waits = []
        self.signaled = False
        self.dsem = None
        self.dval = 0


class Prog:
    ENGS = ("pe", "act", "dve", "pool", "sp")

    def __init__(self):
        self.ops = {e: [] for e in self.ENGS}
        self.known = {e: {} for e in self.ENGS}
        self.dsems = []
        self.bar = {e: [] for e in self.ENGS}

    def dma_sem(self, name):
        s = DmaSem(name)
        self.dsems.append(s)
        return s

    def barrier(self, engs=("pe", "act", "dve", "pool")):
        last = [self.ops[e][-1] for e in engs if self.ops[e]]
        for e in engs:
            self.bar[e] = list(last)

    def _dep(self, op, d):
        if d is None or d is op:
            return
        if d.dsem is None:
            if d.eng == op.eng:
                return
            key = d.eng
            val = d.idx
        else:
            key = d.dsem
            val = d.dval
        k = self.known[op.eng]
        if k.get(key, -1) >= val:
            return
        k[key] = val
        op.waits.append(d)
        d.signaled = True

    def _same(self, o, d):
        k = self.known[o.eng]
        if k.get(o.eng, -1) < d.idx:
            k[o.eng] = d.idx
            o.waits.append(d)
            d.signaled = True

    def op(self, eng, fn, reads=(), writes=(), dsem=None):
        lst = self.ops[eng]
        o = Op(eng, len(lst), fn)
        if dsem is not None:
            dsem.count += 16
            o.dsem = dsem
            o.dval = dsem.count
        is_async = dsem is not None
        if self.bar[eng]:
            for d in self.bar[eng]:
                self._dep(o, d)
            self.bar[eng] = []
        for b in reads:
            d = b.lw
            if d is not None:
                if d.dsem is None and d.eng == eng:
                    if eng != "pe" or is_async:
                        self._same(o, d)
                else:
                    self._dep(o, d)
        for b in writes:
            d = b.lw
            if d is not None:
                if d.dsem is None and d.eng == eng:
                    if is_async:
                        self._same(o, d)
                else:
                    self._dep(o, d)
            if b.rd:
                best = {}
                for r in b.rd:
                    key = r.eng if r.dsem is None else r.dsem
                    v = r.idx if r.dsem is None else r.dval
                    if key not in best or best[key][0] < v:
                        best[key] = (v, r)
                for v, r in best.values():
                    if r.dsem is None and r.eng == eng:
                        if is_async:
                            self._same(o, r)
                    else:
                        self._dep(o, r)
        for b in writes:
            b.lw = o
            b.rd = []
        for b in reads:
            if b.lw is not o and not b.const:
                b.rd.append(o)
        lst.append(o)
        return o

    def emit(self, nc, es, final_waits=()):
        sems = {}
        for e in self.ENGS:
            cnt = 0
            ep = 0
            for o in self.ops[e]:
                if o.signaled and o.dsem is None:
                    if cnt >= EPOCH:
                        ep += 1
                        cnt = 0
                    cnt += 1
                    o.sig_epoch = ep
                    o.sig_val = cnt
            sems[e] = [es.enter_context(nc.semaphore(f"s_{e}_{i}")) for i in range(ep + 1)]
        for s in self.dsems:
            s.handle = es.enter_context(nc.semaphore(f"d_{s.name}"))
        block = es.enter_context(nc.Block())
        hw = {"pe": "tensor", "act": "scalar", "dve": "vector", "pool": "gpsimd", "sp": "sync"}
        prog = self

        def run(e, eng):
            for o in prog.ops[e]:
                for d in o.waits:
                    if d.dsem is None:
                        eng.wait_ge(sems[d.eng][d.sig_epoch], d.sig_val)
                    else:
                        eng.wait_ge(d.dsem.handle, d.dval)
                ins = o.fn(eng)
                if o.dsem is not None:
                    ins.then_inc(o.dsem.handle, 16)
                elif o.signaled:
                    ins.then_inc(sems[e][o.sig_epoch], 1)
            if e == "sp":
                for d in final_waits:
                    eng.wait_ge(d.dsem.handle, d.dval)

        for e in self.ENGS:
            getattr(block, hw[e])(lambda eng, e=e: run(e, eng))

    def stats(self):
        return {e: (len(self.ops[e]), sum(1 for o in self.ops[e] if o.signaled),
                    sum(len(o.waits) for o in self.ops[e])) for e in self.ENGS}


POOL_W = (2, 4, 8, 16)
GAMMA = [1.0 - 2.0 ** (-5.0 - h) for h in range(4)]
SCL = 128.0 ** -0.5

VP_LAYOUT = {}


def _vp_layout():
    off = 0
    for name, n in (("mixer_g", 32), ("ffn_g", 32), ("final_g", 8), ("ret_g", 8), ("pool_scale", 8),
                    ("b_pw1", 32), ("b_dw", 16), ("ln_g", 16), ("ln_b", 16), ("b_pw2", 16),
                    ("wdw", 2 * 8 * 31), ("invf", 1), ("sign", 1), ("kdec", 4)):
        VP_LAYOUT[name] = off
        off += n
    return off


NV = _vp_layout()


def const_tables():
    m = np.arange(128, dtype=np.float64)
    maskT = np.zeros((128, 4, 128), np.float64)
    qdec = np.zeros((128, 4, 128), np.float64)
    cdt = np.zeros((128, 4, 128), np.float64)
    kdec = np.zeros((128, 4), np.float64)
    for h in range(4):
        g = GAMMA[h]
        maskT[:, h, :] = SCL * (g ** (-m[:, None] - 1.0)) * (m[None, :] >= m[:, None])
        qdec[:, h, :] = (g ** (m + 1.0))[None, :]
        cdt[:, h, :] = g ** 128.0
        kdec[:, h] = SCL * g ** (127.0 - m)
    invcnt = np.zeros((128, 4, 16), np.float64)
    t = np.arange(16)
    for gi, w in enumerate(POOL_W):
        invcnt[:, gi, :] = 1.0 / np.minimum(t + 1, w)[None, :]
    half = 64
    invf = (np.float32(10000.0) ** (-(np.arange(half, dtype=np.float32)) / np.float32(half))).astype(np.float32)
    invf128 = np.concatenate([invf, invf])
    sign = np.concatenate([-np.ones(64), np.ones(64)])
    tabs = np.concatenate([maskT.reshape(128, 512), qdec.reshape(128, 512), cdt.reshape(128, 512),
                           np.eye(128), np.full((128, 128), 1.0 / 128.0), invcnt.reshape(128, 64)], axis=1)
    return tabs.astype(np.float32), invf128.astype(np.float32), sign.astype(np.float32), kdec.astype(np.float32)


NTAB = 512 * 3 + 128 + 128 + 64


def build(seq=SEQ, depth=DEPTH, final_norm=True):
    NP = seq // T
    NS = T // 512
    NCH = T // 128
    nc = bass.Bass("TRN2", target_bir_lowering=False)
    dr = lambda name, shape, dt=F32, kind="ExternalInput": nc.dram_tensor(name, shape, dt, kind=kind).ap()
    xT = dr("xT", [D, seq])
    pos = dr("pos", [1, seq], I32)
    vp_d = dr("vp", [128, NV])
    tab_d = dr("tabs", [128, NTAB])
    wgu_d = dr("wgu", [DEPTH, NJ, 128, 2048])
    wd_d = dr("wd", [DEPTH, 8, 128, NJ * 128])
    win_d = dr("win", [2, 24, 128, 1024])
    winv_d = dr("winv", [2, 128, 4096])
    wout_d = dr("wout", [2, 8, 128, 1024])
    poolw_d = dr("poolw", [2, 128, 512])
    pw1_d = dr("pw1", [2, 16, 128, 1024])
    pw2_d = dr("pw2", [2, 8, 128, 1024])
    yT = dr("yT", [D, seq], F32, "ExternalOutput")

    es = ExitStack()
    P = Prog()
    sbt = lambda name, shape, dt: es.enter_context(nc.sbuf_tensor(name, shape, dt))

    h = sbt("h", [128, 8, T], F32)
    hn = sbt("hn", [128, 8, T], BF16)
    NSLOT = 4
    slots = [sbt(f"slot{i}", [128, 4096], BF16) for i in range(NSLOT)]
    ctab = sbt("ctab", [128, T], F32)
    stab = sbt("stab", [128, T], F32)
    vp = sbt("vp_sb", [128, NV], F32)
    maskT = sbt("maskT", [128, 512], F32)
    qdec = sbt("qdec", [128, 512], F32)
    cdt = sbt("cdt", [128, 512], F32)
    identf = sbt("identf", [128, 128], F32)
    onesf = sbt("onesf", [128, 128], F32)
    invcnt = sbt("invcnt", [128, 64], F32)
    identb = sbt("identb", [128, 128], BF16)
    onesb = sbt("onesb", [128, 128], BF16)
    state = [sbt(f"state{i}", [128, 512], F32) for i in range(2)]
    stbf = [[sbt(f"stbf{i}_{k}", [128, 512], BF16) for k in range(2)] for i in range(2)]
    halo_u = [sbt(f"halo_u{i}", [128, 4, 16], F32) for i in range(2)]
    halo_c = [sbt(f"halo_c{i}", [128, 8, 30], BF16) for i in range(2)]
    rs = sbt("rs", [128, 512], F32)
    rstd = sbt("rstd", [128, 512], F32)
    sq = sbt("sq", [128, 8, 512], BF16)
    ARENA = 83968
    arena = sbt("arena", [128, ARENA // 2], BF16)

    def carve(off, shape, dt):
        n = int(np.prod(shape))
        esz = 4 if dt == F32 or dt == I32 else 2
        assert off % 4 == 0 and off + n * esz <= ARENA, (off, shape)
        a = arena[:, off // 2: off // 2 + n * esz // 2]
        if esz == 4:
            a = a.bitcast(dt)
        if len(shape) == 2:
            return a.rearrange("p (a b) -> p a b", a=shape[0])
        if len(shape) == 3:
            return a.rearrange("p (a b c) -> p a b c", a=shape[0], b=shape[1])
        return a

    psf = [es.enter_context(nc.psum_tensor(f"ps{i}", [128, 512], F32)) for i in range(8)]
    ps = [p[:] for p in psf]
    psT = ps[7].bitcast(BF16)

    def mk(name, const=False):
        return Buf(name, const)
    b_h = [[mk(f"h{c}_{s}") for s in range(NS)] for c in range(8)]
    b_hn = [mk(f"hn{s}") for s in range(NS)]
    b_slot = [mk(f"slot{i}") for i in range(NSLOT)]
    d_slot = [P.dma_sem(f"slot{i}") for i in range(NSLOT)]
    b_ps = [mk(f"ps{i}") for i in range(8)]
    b_const = mk("const", True)
    b_ctab = mk("ctab"); b_stab = mk("stab")
    b_state = [mk(f"state{i}") for i in range(2)]
    b_stbf = [[mk(f"stbf{i}_{k}") for k in range(2)] for i in range(2)]
    b_halo_u = [mk(f"halo_u{i}") for i in range(2)]
    b_halo_c = [mk(f"halo_c{i}") for i in range(2)]
    b_rs = mk("rs"); b_rstd = mk("rstd"); b_sq = mk("sq")
    d_x = P.dma_sem("x"); d_c = P.dma_sem("c"); d_o = P.dma_sem("o"); d_pos = P.dma_sem("pos")

    def vcol(name, i=0):
        o = VP_LAYOUT[name] + i
        return vp[:, o:o + 1]

    P.op("sp", lambda e: e.dma_start(out=vp[:], in_=vp_d), writes=[b_const], dsem=d_c)
    for dst, a, b in ((maskT, 0, 512), (qdec, 512, 1024), (cdt, 1024, 1536), (identf, 1536, 1664),
                      (onesf, 1664, 1792), (invcnt, 1792, 1856)):
        P.op("sp", lambda e, dst=dst, a=a, b=b: e.dma_start(out=dst[:], in_=tab_d[:, a:b]),
             writes=[b_const], dsem=d_c)
    P.op("pool", lambda e: e.memset(onesb[:], 1.0), writes=[b_const])
    P.op("pool", lambda e: e.tensor_copy(out=identb[:], in_=identf[:]), reads=[b_const], writes=[b_const])
    for i in range(2):
        P.op("pool", lambda e, i=i: e.memset(state[i][:], 0.0), writes=[b_state[i]])
        P.op("pool", lambda e, i=i: e.memset(stbf[i][0][:], 0.0), writes=[b_stbf[i][0]])
        P.op("pool", lambda e, i=i: e.memset(halo_u[i][:], 0.0), writes=[b_halo_u[i]])
        P.op("pool", lambda e, i=i: e.memset(halo_c[i][:], 0.0), writes=[b_halo_c[i]])

    slot_ctr = [0]

    def load_w(dram_ap, nelem):
        i = slot_ctr[0] % NSLOT
        slot_ctr[0] += 1
        P.op("pool", lambda e: e.dma_start(out=slots[i][:, 0:nelem], in_=dram_ap),
             writes=[b_slot[i]], dsem=d_slot[i])
        return i

    def units(si, n):
        return slots[si][:, 0:n * 1024].rearrange("p (u k n) -> p u k n", u=n, k=8)

    def rmsnorm(gname, layer, out_f32_inplace=False):
        for s in range(NS):
            ts = slice(s * 512, (s + 1) * 512)
            P.op("act", lambda e, ts=ts: e.activation(out=sq[:], in_=h[:, :, ts], func=AF.Square),
                 reads=[b_h[c][s] for c in range(8)], writes=[b_sq])
            for c in range(8):
                P.op("pe", lambda e, c=c: e.matmul(ps[6], lhsT=onesb[:], rhs=sq[:, c, :],
                                                   start=(c == 0), stop=(c == 7)),
                     reads=[b_const, b_sq], writes=[b_ps[6]])
            P.op("act", lambda e: e.activation(out=rs[:], in_=ps[6], func=AF.Sqrt, bias=EPS, scale=1.0 / D),
                 reads=[b_ps[6]], writes=[b_rs])
            P.op("dve", lambda e: e.reciprocal(out=rstd[:], in_=rs[:]), reads=[b_rs], writes=[b_rstd])
            for c in range(8):
                gc = vcol(gname, layer * 8 + c)
                if out_f32_inplace:
                    P.op("dve", lambda e, c=c, ts=ts, gc=gc: e.scalar_tensor_tensor(
                        out=h[:, c, ts], in0=h[:, c, ts], scalar=gc, in1=rstd[:], op0=ALU.mult, op1=ALU.mult),
                        reads=[b_h[c][s], b_const, b_rstd], writes=[b_h[c][s]])
                else:
                    P.op("dve", lambda e, c=c, ts=ts, gc=gc: e.scalar_tensor_tensor(
                        out=hn[:, c, ts], in0=h[:, c, ts], scalar=gc, in1=rstd[:], op0=ALU.mult, op1=ALU.mult),
                        reads=[b_h[c][s], b_const, b_rstd], writes=[b_hn[s]])

    def resid_proj(load_fn, src, b_src, nk, bias_name=None, bias_base=0):
        k = 0
        sv = None
        for c in range(8):
            si, sv, uidx = load_fn(c)
            for s in range(NS):
                ts = slice(s * 512, (s + 1) * 512)
                pp = 4 + (k % 2); k += 1
                for j in range(nk):
                    P.op("pe", lambda e, pp=pp, j=j, ts=ts, sv=sv, uidx=uidx: e.matmul(
                        ps[pp], lhsT=sv[:, uidx, j, :], rhs=src[:, j, ts], start=(j == 0), stop=(j == nk - 1)),
                        reads=[b_slot[si], b_src(j, s)], writes=[b_ps[pp]])
                if bias_name is None:
                    P.op("dve", lambda e, pp=pp, c=c, ts=ts: e.tensor_tensor(
                        out=h[:, c, ts], in0=h[:, c, ts], in1=ps[pp], op=ALU.add),
                        reads=[b_h[c][s], b_ps[pp]], writes=[b_h[c][s]])
                else:
                    bc = vcol(bias_name, bias_base + c)
                    P.op("dve", lambda e, pp=pp, c=c, ts=ts, bc=bc: e.scalar_tensor_tensor(
                        out=h[:, c, ts], in0=ps[pp], scalar=bc, in1=h[:, c, ts], op0=ALU.add, op1=ALU.add),
                        reads=[b_h[c][s], b_ps[pp], b_const], writes=[b_h[c][s]])

    def ffn(layer):
        act = carve(0, [NJ, T], BF16)
        sgt = [carve(45056 + i * 2048, [512], F32) for i in range(2)]
        b_act = [[mk(f"act{j}_{s}") for s in range(NS)] for j in range(NJ)]
        b_sg = [mk(f"sg{i}") for i in range(2)]
        rmsnorm("ffn_g", layer)
        k = 0
        for j2 in range(NJ // 2):
            si = load_w(wgu_d[layer, 2 * j2:2 * j2 + 2].rearrange("j p n -> p j n"), 4096)
            sv = slots[si][:].rearrange("p (j g k n) -> p j g k n", j=2, g=2, k=8)
            for jj in range(2):
                j = 2 * j2 + jj
                for s in range(NS):
                    ts = slice(s * 512, (s + 1) * 512)
                    pa = (k % 2) * 2; pb = pa + 1; sgi = k % 2; k += 1
                    for g, pp in ((0, pa), (1, pb)):
                        for kc in range(8):
                            P.op("pe", lambda e, g=g, pp=pp, kc=kc, jj=jj, ts=ts, sv=sv: e.matmul(
                                ps[pp], lhsT=sv[:, jj, g, kc, :], rhs=hn[:, kc, ts],
                                start=(kc == 0), stop=(kc == 7)),
                                reads=[b_slot[si], b_hn[s]], writes=[b_ps[pp]])
                    P.op("act", lambda e, pa=pa, sgi=sgi: e.activation(out=sgt[sgi], in_=ps[pa], func=AF.Silu),
                         reads=[b_ps[pa]], writes=[b_sg[sgi]])
                    P.op("dve", lambda e, pb=pb, sgi=sgi, j=j, ts=ts: e.tensor_tensor(
                        out=act[:, j, ts], in0=sgt[sgi], in1=ps[pb], op=ALU.mult),
                        reads=[b_sg[sgi], b_ps[pb]], writes=[b_act[j][s]])

        def ld(c):
            si = load_w(wd_d[layer, c], NJ * 128)
            sv = slots[si][:, 0:NJ * 128].rearrange("p (u j n) -> p u j n", u=1, j=NJ)
            return si, sv, 0
        resid_proj(ld, act, lambda j, s: b_act[j][s], NJ)

    def rope_tables(p0):
        pos_i = carve(0, [T], I32)
        t0 = carve(4096, [T], F32)
        yv = carve(8192, [T], F32)
        n_i = carve(12288, [T], I32)
        r = carve(16384, [T], F32)
        m = carve(20480, [T], F32)
        rc = carve(24576, [T], F32)
        b = {n: mk("rt_" + n) for n in ("pos", "t0", "y", "ni", "r", "m", "rc")}
        P.op("sp", lambda e: e.dma_start(out=pos_i, in_=bass.AP(pos.tensor, p0, [[0, 128], [1, T]])),
             writes=[b["pos"]], dsem=d_pos)
        P.op("dve", lambda e: e.tensor_copy(out=t0, in_=pos_i), reads=[b["pos"]], writes=[b["t0"]])
        P.op("dve", lambda e: e.tensor_scalar(out=t0, in0=t0, scalar1=vcol("invf"), scalar2=None, op0=ALU.mult),
             reads=[b["t0"], b_const], writes=[b["t0"]])
        P.op("dve", lambda e: e.tensor_scalar(out=yv, in0=t0, scalar1=1.0 / (2 * PI), scalar2=None, op0=ALU.mult),
             reads=[b["t0"]], writes=[b["y"]])
        P.op("dve", lambda e: e.tensor_copy(out=n_i, in_=yv), reads=[b["y"]], writes=[b["ni"]])
        P.op("dve", lambda e: e.tensor_copy(out=yv, in_=n_i), reads=[b["ni"]], writes=[b["y"]])
        C1 = 6.28125
        C2 = float(2 * np.pi - 6.28125)
        P.op("dve", lambda e: e.scalar_tensor_tensor(out=r, in0=yv, scalar=-C1, in1=t0, op0=ALU.mult, op1=ALU.add),
             reads=[b["y"], b["t0"]], writes=[b["r"]])
        P.op("dve", lambda e: e.scalar_tensor_tensor(out=r, in0=yv, scalar=-C2, in1=r, op0=ALU.mult, op1=ALU.add),
             reads=[b["y"], b["r"]], writes=[b["r"]])

        def wrap(x, bx):
            P.op("dve", lambda e: e.tensor_scalar(out=m, in0=x, scalar1=PI, scalar2=-2 * PI, op0=ALU.is_gt, op1=ALU.mult),
                 reads=[bx], writes=[b["m"]])
            P.op("dve", lambda e: e.tensor_tensor(out=x, in0=x, in1=m, op=ALU.add), reads=[bx, b["m"]], writes=[bx])
            P.op("dve", lambda e: e.tensor_scalar(out=x, in0=x, scalar1=PI, scalar2=-PI, op0=ALU.min, op1=ALU.max),
                 reads=[bx], writes=[bx])
        wrap(r, b["r"])
        P.op("dve", lambda e: e.tensor_scalar(out=rc, in0=r, scalar1=PI / 2, scalar2=None, op0=ALU.add),
             reads=[b["r"]], writes=[b["rc"]])
        wrap(rc, b["rc"])
        P.op("act", lambda e: e.activation(out=stab[:], in_=r, func=AF.Sin, scale=vcol("sign")),
             reads=[b["r"], b_const], writes=[b_stab])
        P.op("act", lambda e: e.activation(out=ctab[:], in_=rc, func=AF.Sin), reads=[b["rc"]], writes=[b_ctab])

    def even_mixer(layer, pidx):
        i2 = layer // 2
        qr = carve(0, [4, T], BF16)
        kr = carve(8192, [4, T], BF16)
        v_tok = carve(16384, [NCH, 512], BF16)
        sgn = carve(24576, [NCH, 4, 128], BF16)
        mix = carve(32768, [8, T], BF16)
        rt1 = [carve(49152 + i * 2048, [512], F32) for i in range(2)]
        rt2 = [carve(53248 + i * 2048, [512], F32) for i in range(2)]
        ub = [carve(57344 + i * 4160, [16 + T], F32) for i in range(2)]
        pt = [carve(65664 + i * 4160, [16 + T], F32) for i in range(2)]
        yb = [carve(73984 + i * 2048, [T], BF16) for i in range(2)]
        gt = [carve(78080 + i * 2048, [512], F32) for i in range(2)]
        kd_tok = carve(49152, [NCH, 512], BF16)
        Sm = [carve(57344 + i * 1024, [512], BF16) for i in range(2)]
        o_sb, osq, msq, var, sd, rinv, xc = [carve(59392 + i * 2048, [512], F32) for i in range(7)]

        b_qr = [[mk(f"qr{hh}_{s}") for s in range(NS)] for hh in range(4)]
        b_kr = [[mk(f"kr{hh}_{s}") for s in range(NS)] for hh in range(4)]
        b_v = [mk(f"v{n}") for n in range(NCH)]
        b_sgn = [[mk(f"sgn{hh}_{s}") for s in range(NS)] for hh in range(4)]
        b_mix = [[mk(f"mix{j}_{s}") for s in range(NS)] for j in range(8)]
        b_rt1 = [mk("rt1a"), mk("rt1b")]; b_rt2 = [mk("rt2a"), mk("rt2b")]
        b_ub = [mk("ub0"), mk("ub1")]; b_pt = [mk("pt0"), mk("pt1")]; b_yb = [mk("yb0"), mk("yb1")]
        b_gt = [mk("gt0"), mk("gt1")]
        b_kd = [mk(f"kd{n}") for n in range(NCH)]
        b_Sm = [mk("Sm0"), mk("Sm1")]
        b_t = {n: mk("t_" + n) for n in ("o_sb", "osq", "msq", "var", "sd", "rinv", "xc")}

        rmsnorm("mixer_g", layer)
        P.barrier()
        kk = [0]

        def proj_pair(sv, ua, ub_, s):
            ts = slice(s * 512, (s + 1) * 512)
            pa = (kk[0] % 2) * 2; pb = pa + 1; kk[0] += 1
            for u, pp in ((ua, pa), (ub_, pb)):
                if u is None:
                    continue
                for kc in range(8):
                    P.op("pe", lambda e, u=u, pp=pp, kc=kc, ts=ts, sv=sv: e.matmul(
                        ps[pp], lhsT=sv[:, u, kc, :], rhs=hn[:, kc, ts], start=(kc == 0), stop=(kc == 7)),
                        reads=[b_slot[cur_slot[0]], b_hn[s]], writes=[b_ps[pp]])
            return pa, pb

        cur_slot = [0]
        ri = 0
        for x, dst, bdst in ((0, qr, b_qr), (1, kr, b_kr)):
            for hp in range(2):
                si = load_w(win_d[i2, (x * 2 + hp) * 4:(x * 2 + hp) * 4 + 4].rearrange("u p n -> p u n"), 4096)
                cur_slot[0] = si
                sv = units(si, 4)
                for hl in range(2):
                    hh = hp * 2 + hl
                    for s in range(NS):
                        ts = slice(s * 512, (s + 1) * 512)
                        pa, pb = proj_pair(sv, hl * 2, hl * 2 + 1, s)
                        r1 = ri % 2; ri += 1
                        P.op("dve", lambda e, pa=pa, r1=r1, ts=ts: e.tensor_tensor(
                            out=rt1[r1], in0=ps[pa], in1=ctab[:, ts], op=ALU.mult),
                            reads=[b_ps[pa], b_ctab], writes=[b_rt1[r1]])
                        P.op("dve", lambda e, pb=pb, r1=r1, ts=ts: e.tensor_tensor(
                            out=rt2[r1], in0=ps[pb], in1=stab[:, ts], op=ALU.mult),
                            reads=[b_ps[pb], b_stab], writes=[b_rt2[r1]])
                        P.op("pool", lambda e, r1=r1, dst=dst, hh=hh, ts=ts: e.tensor_tensor(
                            out=dst[:, hh, ts], in0=rt1[r1], in1=rt2[r1], op=ALU.add),
                            reads=[b_rt1[r1], b_rt2[r1]], writes=[bdst[hh][s]])
        si = load_w(win_d[i2, 16:20].rearrange("u p n -> p u n"), 4096)
        cur_slot[0] = si
        sv = units(si, 4)
        gi_ = 0
        for hh in range(4):
            for s in range(NS):
                pa, _ = proj_pair(sv, hh, None, s)
                g1 = gi_ % 2; gi_ += 1
                P.op("act", lambda e, pa=pa, g1=g1: e.activation(out=gt[g1], in_=ps[pa], func=AF.Silu),
                     reads=[b_ps[pa]], writes=[b_gt[g1]])
                P.op("pool", lambda e, g1=g1, hh=hh, s=s: e.tensor_scalar(
                    out=sgn[:, s * 4:(s + 1) * 4, hh, :], in0=gt[g1].rearrange("p (n c) -> p n c", n=4),
                    scalar1=vcol("ret_g", i2 * 4 + hh), scalar2=None, op0=ALU.mult),
                    reads=[b_gt[g1], b_const], writes=[b_sgn[hh][s]])
        si = load_w(winv_d[i2], 4096)
        svv = slots[si][:].rearrange("p (k n) -> p k n", k=8)
        for n in range(NCH):
            pp = 4 + (n % 2)
            tn = slice(n * 128, (n + 1) * 128)
            for kc in range(8):
                P.op("pe", lambda e, pp=pp, kc=kc, tn=tn, svv=svv: e.matmul(
                    ps[pp], lhsT=hn[:, kc, tn], rhs=svv[:, kc, :], start=(kc == 0), stop=(kc == 7)),
                    reads=[b_slot[si], b_hn[n // 4]], writes=[b_ps[pp]])
            P.op("act", lambda e, pp=pp, n=n: e.activation(out=v_tok[:, n, :], in_=ps[pp], func=AF.Copy),
                 reads=[b_ps[pp]], writes=[b_v[n]])
        si_u = load_w(win_d[i2, 20:24].rearrange("u p n -> p u n"), 4096)
        si_pw = load_w(poolw_d[i2], 512)
        svu = units(si_u, 4)
        pw_v = slots[si_pw][:, 0:512].rearrange("p (g n) -> p g n", g=4)
        for g in range(4):
            w = POOL_W[g]
            u1 = g % 2
            U = ub[u1]
            P.op("pool", lambda e, U=U, g=g: e.tensor_copy(out=U[:, 0:16], in_=halo_u[i2][:, g, :]),
                 reads=[b_halo_u[i2]], writes=[b_ub[u1]])
            for s in range(NS):
                cur_slot[0] = si_u
                pa, _ = proj_pair(svu, g, None, s)
                P.op("act", lambda e, pa=pa, U=U, s=s: e.activation(
                    out=U[:, 16 + s * 512:16 + (s + 1) * 512], in_=ps[pa], func=AF.Copy),
                    reads=[b_ps[pa]], writes=[b_ub[u1]])
            P.op("pool", lambda e, U=U, g=g: e.tensor_copy(out=halo_u[i2][:, g, :], in_=U[:, T:T + 16]),
                 reads=[b_ub[u1]], writes=[b_halo_u[i2]])
            src, bsrc = U, b_ub[u1]
            sh = 1
            lo = 0
            k = 0
            while sh < w:
                dst, bd = pt[k % 2], b_pt[k % 2]
                lo2 = lo + sh
                P.op("pool", lambda e, src=src, dst=dst, lo2=lo2, sh=sh: e.tensor_tensor(
                    out=dst[:, lo2:16 + T], in0=src[:, lo2:16 + T], in1=src[:, lo2 - sh:16 + T - sh], op=ALU.add),
                    reads=[bsrc], writes=[bd])
                src, bsrc = dst, bd
                lo = lo2
                sh *= 2
                k += 1
            Y = yb[u1]
            P.op("dve", lambda e, src=src, U=U, Y=Y, w=w: e.scalar_tensor_tensor(
                out=Y, in0=src[:, 16:16 + T], scalar=1.0 / w, in1=U[:, 16:16 + T], op0=ALU.mult, op1=ALU.subtract),
                reads=[bsrc, b_ub[u1]], writes=[b_yb[u1]])
            if pidx == 0:
                P.op("dve", lambda e, src=src, g=g: e.tensor_tensor(
                    out=src[:, 16:32], in0=src[:, 16:32], in1=invcnt[:, g * 16:(g + 1) * 16], op=ALU.mult),
                    reads=[bsrc, b_const, b_yb[u1]], writes=[bsrc])
                P.op("dve", lambda e, src=src, U=U, Y=Y: e.tensor_tensor(
                    out=Y[:, 0:16], in0=src[:, 16:32], in1=U[:, 16:32], op=ALU.subtract),
                    reads=[bsrc, b_ub[u1]], writes=[b_yb[u1]])
            for s in range(NS):
                ts = slice(s * 512, (s + 1) * 512)
                pp = 4 + (s % 2)
                P.op("pe", lambda e, pp=pp, g=g, Y=Y, ts=ts: e.matmul(
                    ps[pp], lhsT=pw_v[:, g, :], rhs=Y[:, ts], start=True, stop=True),
                    reads=[b_slot[si_pw], b_yb[u1]], writes=[b_ps[pp]])
                P.op("act", lambda e, pp=pp, g=g, ts=ts: e.activation(
                    out=mix[:, 4 + g, ts], in_=ps[pp], func=AF.Identity, scale=vcol("pool_scale", i2 * 4 + g)),
                    reads=[b_ps[pp], b_const], writes=[b_mix[4 + g][s]])
        P.barrier()
        psKV, psS, psO, psM, psE = 0, (1, 2), (3, 4), 5, 6
        st, bst = state[i2], b_state[i2]
        for n in range(NCH):
            s = n // 4
            tn = slice(n * 128, (n + 1) * 128)
            cb = n % 2
            for hh in range(4):
                P.op("pe", lambda e, hh=hh, tn=tn: e.transpose(
                    out=psT[:, hh * 128:(hh + 1) * 128], in_=kr[:, hh, tn], identity=identb[:]),
                    reads=[b_kr[hh][s], b_const], writes=[b_ps[7]])
            for hh in range(4):
                P.op("act", lambda e, hh=hh, n=n: e.activation(
                    out=kd_tok[:, n, hh * 128:(hh + 1) * 128], in_=psT[:, hh * 128:(hh + 1) * 128],
                    func=AF.Identity, scale=vcol("kdec", hh)), reads=[b_ps[7], b_const], writes=[b_kd[n]])
            pS = psS[n % 2]
            for hh in range(4):
                P.op("pe", lambda e, hh=hh, tn=tn, pS=pS: e.matmul(
                    ps[pS][:, hh * 128:(hh + 1) * 128], lhsT=kr[:, hh, tn], rhs=qr[:, hh, tn], start=True, stop=True),
                    reads=[b_kr[hh][s], b_qr[hh][s]], writes=[b_ps[pS]])
            P.op("dve", lambda e, pS=pS, n=n: e.tensor_tensor(out=Sm[n % 2], in0=ps[pS], in1=maskT[:], op=ALU.mult),
                 reads=[b_ps[pS], b_const], writes=[b_Sm[n % 2]])
            pO = psO[n % 2]
            for hh in range(4):
                hs = slice(hh * 128, (hh + 1) * 128)
                P.op("pe", lambda e, hs=hs, n=n, pO=pO: e.matmul(
                    ps[pO][:, hs], lhsT=v_tok[:, n, hs], rhs=Sm[n % 2][:, hs], start=True, stop=False),
                    reads=[b_v[n], b_Sm[n % 2]], writes=[b_ps[pO]])
                P.op("pe", lambda e, hs=hs, hh=hh, tn=tn, pO=pO, cb=cb: e.matmul(
                    ps[pO][:, hs], lhsT=stbf[i2][cb][:, hs], rhs=qr[:, hh, tn], start=False, stop=True),
                    reads=[b_stbf[i2][cb], b_qr[hh][s]], writes=[b_ps[pO]])
            for hh in range(4):
                hs = slice(hh * 128, (hh + 1) * 128)
                P.op("pe", lambda e, hs=hs, n=n: e.matmul(
                    ps[psKV][:, hs], lhsT=kd_tok[:, n, hs], rhs=v_tok[:, n, hs], start=True, stop=True),
                    reads=[b_kd[n], b_v[n]], writes=[b_ps[psKV]])
            P.op("dve", lambda e, st=st: e.tensor_tensor(out=st[:], in0=st[:], in1=cdt[:], op=ALU.mult),
                 reads=[bst, b_const], writes=[bst])
            P.op("dve", lambda e, st=st: e.tensor_tensor(out=st[:], in0=st[:], in1=ps[psKV], op=ALU.add),
                 reads=[bst, b_ps[psKV]], writes=[bst])
            P.op("act", lambda e, st=st, cb=cb: e.activation(out=stbf[i2][1 - cb][:], in_=st[:], func=AF.Copy),
                 reads=[bst], writes=[b_stbf[i2][1 - cb]])
            P.op("dve", lambda e, pO=pO: e.tensor_tensor(out=o_sb, in0=ps[pO], in1=qdec[:], op=ALU.mult),
                 reads=[b_ps[pO], b_const], writes=[b_t["o_sb"]])
            P.op("act", lambda e: e.activation(out=osq, in_=o_sb, func=AF.Square),
                 reads=[b_t["o_sb"]], writes=[b_t["osq"]])
            P.op("pe", lambda e: e.matmul(ps[psM], lhsT=onesf[:], rhs=o_sb, start=True, stop=True),
                 reads=[b_const, b_t["o_sb"]], writes=[b_ps[psM]])
            P.op("pe", lambda e: e.matmul(ps[psE], lhsT=onesf[:], rhs=osq, start=True, stop=True),
                 reads=[b_const, b_t["osq"]], writes=[b_ps[psE]])
            P.op("act", lambda e: e.activation(out=msq, in_=ps[psM], func=AF.Square),
                 reads=[b_ps[psM]], writes=[b_t["msq"]])
            P.op("dve", lambda e: e.tensor_tensor(out=var, in0=ps[psE], in1=msq, op=ALU.subtract),
                 reads=[b_ps[psE], b_t["msq"]], writes=[b_t["var"]])
            P.op("act", lambda e: e.activation(out=sd, in_=var, func=AF.Sqrt, bias=EPS, scale=1.0),
                 reads=[b_t["var"]], writes=[b_t["sd"]])
            P.op("dve", lambda e: e.reciprocal(out=rinv, in_=sd), reads=[b_t["sd"]], writes=[b_t["rinv"]])
            P.op("dve", lambda e: e.tensor_tensor(out=xc, in0=o_sb, in1=ps[psM], op=ALU.subtract),
                 reads=[b_t["o_sb"], b_ps[psM]], writes=[b_t["xc"]])
            P.op("pool", lambda e: e.tensor_tensor(out=xc, in0=xc, in1=rinv, op=ALU.mult),
                 reads=[b_t["xc"], b_t["rinv"]], writes=[b_t["xc"]])
            P.op("pool", lambda e, n=n, tn=tn: e.tensor_tensor(
                out=mix[:, 0:4, tn], in0=xc.rearrange("p (h c) -> p h c", h=4), in1=sgn[:, n, :, :], op=ALU.mult),
                reads=[b_t["xc"]] + [b_sgn[hh][s] for hh in range(4)],
                writes=[b_mix[hh][s] for hh in range(4)])
        def ld(c):
            if c % 4 == 0:
                ld.si = load_w(wout_d[i2, c:c + 4].rearrange("u p n -> p u n"), 4096)
                ld.sv = units(ld.si, 4)
            return ld.si, ld.sv, c % 4
        resid_proj(ld, mix, lambda j, s: b_mix[j][s], 8)

    def odd_mixer(layer, pidx):
        i2 = layer // 2
        U = carve(0, [8, 30 + T], BF16)
        dg = [carve(16864 + i * 7936, [31, 128], BF16) for i in range(2)]
        dw = carve(32736, [8, T], F32)
        sig = [carve(65504 + i * 2048, [512], F32) for i in range(2)]
        dwb = [carve(69600 + i * 1024, [512], BF16) for i in range(2)]
        sqb = [carve(71648 + i * 1024, [512], BF16) for i in range(2)]
        mean, msq, var, sd, rinv = [carve(73696 + i * 2048, [512], F32) for i in range(5)]
        b_U = [[mk(f"U{c}_{s}") for s in range(NS)] for c in range(8)]
        b_Uh = [mk(f"Uh{c}") for c in range(8)]
        b_dg = [mk("dg0"), mk("dg1")]
        b_dw = [[mk(f"dw{c}_{s}") for s in range(NS)] for c in range(8)]
        b_sig = [mk("sig0"), mk("sig1")]
        b_dwb = [mk("dwb0"), mk("dwb1")]; b_sqb = [mk("sqb0"), mk("sqb1")]
        b_t = {n: mk("ot_" + n) for n in ("mean", "msq", "var", "sd", "rinv")}

        rmsnorm("mixer_g", layer)
        P.barrier()
        for c in range(8):
            P.op("pool", lambda e, c=c: e.tensor_copy(out=U[:, c, 0:30], in_=halo_c[i2][:, c, :]),
                 reads=[b_halo_c[i2]], writes=[b_Uh[c]])
        kk = 0
        for c in range(8):
            if c % 2 == 0:
                si = load_w(pw1_d[i2, 2 * c:2 * c + 4].rearrange("u p n -> p u n"), 4096)
                sv = units(si, 4)
            for s in range(NS):
                ts = slice(s * 512, (s + 1) * 512)
                pa = (kk % 2) * 2; pb = pa + 1; sg1 = kk % 2; kk += 1
                for u, pp in (((c % 2) * 2, pa), ((c % 2) * 2 + 1, pb)):
                    for kc in range(8):
                        P.op("pe", lambda e, u=u, pp=pp, kc=kc, ts=ts, sv=sv: e.matmul(
                            ps[pp], lhsT=sv[:, u, kc, :], rhs=hn[:, kc, ts], start=(kc == 0), stop=(kc == 7)),
                            reads=[b_slot[si], b_hn[s]], writes=[b_ps[pp]])
                P.op("act", lambda e, pb=pb, sg1=sg1, c=c: e.activation(
                    out=sig[sg1], in_=ps[pb], func=AF.Sigmoid, bias=vcol("b_pw1", i2 * 16 + 8 + c), scale=1.0),
                    reads=[b_ps[pb], b_const], writes=[b_sig[sg1]])
                P.op("dve", lambda e, pa=pa, sg1=sg1, c=c, s=s: e.scalar_tensor_tensor(
                    out=U[:, c, 30 + s * 512:30 + (s + 1) * 512], in0=ps[pa], scalar=vcol("b_pw1", i2 * 16 + c),
                    in1=sig[sg1], op0=ALU.add, op1=ALU.mult),
                    reads=[b_ps[pa], b_const, b_sig[sg1]], writes=[b_U[c][s]])
        for c in range(8):
            P.op("pool", lambda e, c=c: e.tensor_copy(out=halo_c[i2][:, c, :], in_=U[:, c, T:T + 30]),
                 reads=[b_U[c][NS - 1]], writes=[b_halo_c[i2]])
        psC, psM, psE = (0, 1), (2, 3), (4, 5)
        wd0 = VP_LAYOUT["wdw"] + i2 * 8 * 31
        kk = 0
        for c in range(8):
            d1 = c % 2
            wsl = vp[:, wd0 + c * 31: wd0 + (c + 1) * 31]
            P.op("pool", lambda e, d1=d1, wsl=wsl: e.tensor_tensor(
                out=dg[d1], in0=identf[:].unsqueeze(1).broadcast_to([128, 31, 128]),
                in1=wsl.unsqueeze(2).broadcast_to([128, 31, 128]), op=ALU.mult),
                reads=[b_const], writes=[b_dg[d1]])
            for s in range(NS):
                ts = slice(s * 512, (s + 1) * 512)
                pc = psC[kk % 2]; k1 = kk % 2; kk += 1
                rd = [b_dg[d1], b_U[c][s], b_Uh[c]] + ([b_U[c][s - 1]] if s > 0 else [])
                for j in range(31):
                    P.op("pe", lambda e, pc=pc, d1=d1, j=j, c=c, s=s: e.matmul(
                        ps[pc], lhsT=dg[d1][:, j, :], rhs=U[:, c, s * 512 + j:s * 512 + j + 512],
                        start=(j == 0), stop=(j == 30)), reads=rd, writes=[b_ps[pc]])
                bcol = vcol("b_dw", i2 * 8 + c)
                P.op("act", lambda e, pc=pc, c=c, ts=ts, bcol=bcol: e.activation(
                    out=dw[:, c, ts], in_=ps[pc], func=AF.Identity, bias=bcol, scale=1.0),
                    reads=[b_ps[pc], b_const], writes=[b_dw[c][s]])
                P.op("act", lambda e, pc=pc, k1=k1, bcol=bcol: e.activation(
                    out=sqb[k1], in_=ps[pc], func=AF.Square, bias=bcol, scale=1.0),
                    reads=[b_ps[pc], b_const], writes=[b_sqb[k1]])
                P.op("pool", lambda e, k1=k1, c=c, ts=ts: e.tensor_copy(out=dwb[k1], in_=dw[:, c, ts]),
                     reads=[b_dw[c][s]], writes=[b_dwb[k1]])
                P.op("pe", lambda e, s=s, k1=k1, c=c: e.matmul(
                    ps[psM[s]], lhsT=onesb[:], rhs=dwb[k1], start=(c == 0), stop=(c == 7)),
                    reads=[b_const, b_dwb[k1]], writes=[b_ps[psM[s]]])
                P.op("pe", lambda e, s=s, k1=k1, c=c: e.matmul(
                    ps[psE[s]], lhsT=onesb[:], rhs=sqb[k1], start=(c == 0), stop=(c == 7)),
                    reads=[b_const, b_sqb[k1]], writes=[b_ps[psE[s]]])
        for s in range(NS):
            ts = slice(s * 512, (s + 1) * 512)
            P.op("act", lambda e, s=s: e.activation(out=mean, in_=ps[psM[s]], func=AF.Copy, scale=1.0 / D),
                 reads=[b_ps[psM[s]]], writes=[b_t["mean"]])
            P.op("act", lambda e, s=s: e.activation(out=msq, in_=ps[psM[s]], func=AF.Square, scale=1.0 / D),
                 reads=[b_ps[psM[s]]], writes=[b_t["msq"]])
            P.op("dve", lambda e, s=s: e.scalar_tensor_tensor(
                out=var, in0=ps[psE[s]], scalar=1.0 / D, in1=msq, op0=ALU.mult, op1=ALU.subtract),
                reads=[b_ps[psE[s]], b_t["msq"]], writes=[b_t["var"]])
            P.op("act", lambda e: e.activation(out=sd, in_=var, func=AF.Sqrt, bias=EPS, scale=1.0),
                 reads=[b_t["var"]], writes=[b_t["sd"]])
            P.op("dve", lambda e: e.reciprocal(out=rinv, in_=sd), reads=[b_t["sd"]], writes=[b_t["rinv"]])
            for c in range(8):
                P.op("dve", lambda e, c=c, ts=ts: e.tensor_tensor(out=dw[:, c, ts], in0=dw[:, c, ts], in1=mean, op=ALU.subtract),
                     reads=[b_dw[c][s], b_t["mean"]], writes=[b_dw[c][s]])
                P.op("pool", lambda e, c=c, ts=ts: e.tensor_tensor(out=dw[:, c, ts], in0=dw[:, c, ts], in1=rinv, op=ALU.mult),
                     reads=[b_dw[c][s], b_t["rinv"]], writes=[b_dw[c][s]])
                P.op("act", lambda e, c=c, ts=ts: e.activation(
                    out=hn[:, c, ts], in_=dw[:, c, ts], func=AF.Silu,
                    bias=vcol("ln_b", i2 * 8 + c), scale=vcol("ln_g", i2 * 8 + c)),
                    reads=[b_dw[c][s], b_const], writes=[b_hn[s]])
        def ld(c):
            if c % 4 == 0:
                ld.si = load_w(pw2_d[i2, c:c + 4].rearrange("u p n -> p u n"), 4096)
                ld.sv = units(ld.si, 4)
            return ld.si, ld.sv, c % 4
        resid_proj(ld, hn, lambda j, s: b_hn[s], 8, bias_name="b_pw2", bias_base=i2 * 8)

    xv = xT.rearrange("(c p) t -> p c t", p=128)
    yv = yT.rearrange("(c p) t -> p c t", p=128)
    fin = []
    for pidx in range(NP):
        p0 = pidx * T
        for c in range(8):
            P.op("sp", lambda e, c=c, p0=p0: e.dma_start(out=h[:, c, :], in_=xv[:, c, p0:p0 + T]),
                 writes=[b_h[c][s] for s in range(NS)], dsem=d_x)
        P.barrier()
        rope_tables(p0)
        for layer in range(depth):
            P.barrier()
            if layer % 2 == 0:
                even_mixer(layer, pidx)
            else:
                odd_mixer(layer, pidx)
            P.barrier()
            ffn(layer)
        if final_norm:
            rmsnorm("final_g", 0, out_f32_inplace=True)
        for c in range(8):
            fin.append(P.op("sp", lambda e, c=c, p0=p0: e.dma_start(out=yv[:, c, p0:p0 + T], in_=h[:, c, :]),
                            reads=[b_h[c][s] for s in range(NS)], dsem=d_o))
    P.emit(nc, es, final_waits=fin[-1:])
    es.close()
    return nc, P


def _units(w):
    n = w.shape[1] // 128
    return np.ascontiguousarray(w.reshape(8, 128, n, 128).transpose(2, 1, 0, 3)).reshape(n, 128, 1024)


def prep_weights(inp):
    f = lambda a: np.asarray(a, dtype=np.float32)
    out = {}
    wg = f(inp["ffn_w_gate"]); wu = f(inp["ffn_w_up"]); wdn = f(inp["ffn_w_down"])
    L = wg.shape[0]
    wgu = np.stack([wg.reshape(L, 8, 128, NJ, 128), wu.reshape(L, 8, 128, NJ, 128)], 1)
    out["wgu"] = np.ascontiguousarray(wgu.transpose(0, 4, 3, 1, 2, 5)).reshape(L, NJ, 128, 2048)
    out["wd"] = np.ascontiguousarray(wdn.reshape(L, NJ, 128, 8, 128).transpose(0, 3, 2, 1, 4)).reshape(L, 8, 128, NJ * 128)
    w_in = f(inp["ret_w_in"])
    perm = np.concatenate([np.arange(64, 128), np.arange(0, 64)])
    win = []
    winv = []
    for i in range(w_in.shape[0]):
        W = w_in[i]
        us = {}
        for x, base in (("q", 0), ("k", 512)):
            blk = _units(W[:, base:base + 512])
            rot = _units(W[:, base:base + 512].reshape(D, 4, 128)[:, :, perm].reshape(D, 512))
            us[x] = (blk, rot)
        order = []
        for x in ("q", "k"):
            blk, rot = us[x]
            for hp in range(2):
                for hl in range(2):
                    hh = hp * 2 + hl
                    order += [blk[hh], rot[hh]]
        g_u = _units(W[:, 1536:2048])
        u_u = _units(W[:, 2048:2560])
        order += [g_u[k] for k in range(4)] + [u_u[k] for k in range(4)]
        win.append(np.stack(order, 0))
        winv.append(np.ascontiguousarray(W[:, 1024:1536].reshape(8, 128, 512).transpose(1, 0, 2)).reshape(128, 4096))
    out["win"] = np.stack(win, 0)
    out["winv"] = np.stack(winv, 0)
    wo = f(inp["mix_w_out"])
    out["wout"] = np.stack([_units(wo[i]) for i in range(wo.shape[0])], 0)
    pw = f(inp["pool_w"])
    out["poolw"] = np.ascontiguousarray(pw.transpose(0, 2, 1, 3)).reshape(pw.shape[0], 128, 512)
    w1 = f(inp["conv_w_pw1"])
    p1 = []
    for i in range(w1.shape[0]):
        a_u = _units(w1[i][:, :D]); g_u = _units(w1[i][:, D:])
        o = []
        for c in range(8):
            o += [a_u[c], g_u[c]]
        p1.append(np.stack(o, 0))
    out["pw1"] = np.stack(p1, 0)
    w2 = f(inp["conv_w_pw2"])
    out["pw2"] = np.stack([_units(w2[i]) for i in range(w2.shape[0])], 0)
    vp = np.zeros((128, NV), np.float32)

    def put(name, arr, base=0):
        arr = np.asarray(arr, np.float32)
        n = arr.shape[0] // 128
        vp[:, VP_LAYOUT[name] + base: VP_LAYOUT[name] + base + n] = arr.reshape(n, 128).T
    for l in range(DEPTH):
        put("mixer_g", f(inp["mixer_norm_g"])[l], l * 8)
        put("ffn_g", f(inp["ffn_norm_g"])[l], l * 8)
    put("final_g", f(inp["final_norm_g"]))
    for i in range(2):
        put("ret_g", f(inp["ret_norm_g"])[i], i * 4)
        put("pool_scale", f(inp["pool_scale"])[i], i * 4)
        put("b_pw1", f(inp["conv_b_pw1"])[i], i * 16)
        put("b_dw", f(inp["conv_b_dw"])[i], i * 8)
        put("ln_g", f(inp["conv_ln_g"])[i], i * 8)
        put("ln_b", f(inp["conv_ln_b"])[i], i * 8)
        put("b_pw2", f(inp["conv_b_pw2"])[i], i * 8)
        wdw = f(inp["conv_w_dw"])[i]
        o = VP_LAYOUT["wdw"] + i * 8 * 31
        vp[:, o:o + 248] = wdw.reshape(31, 8, 128).transpose(2, 1, 0).reshape(128, 248)
    tabs, invf, sign, kdec = const_tables()
    vp[:, VP_LAYOUT["invf"]] = invf
    vp[:, VP_LAYOUT["sign"]] = sign
    vp[:, VP_LAYOUT["kdec"]:VP_LAYOUT["kdec"] + 4] = kdec
    out["vp"] = vp
    out["tabs"] = tabs
    return out


_CACHE = {}


def kernel(**inputs):
    x = np.asarray(inputs["x"], dtype=np.float32)
    positions = np.asarray(inputs["positions"]).astype(np.int32)
    B = x.shape[0]
    shared = prep_weights(inputs)
    if "nc" not in _CACHE:
        _CACHE["nc"] = build()[0]
    nc = _CACHE["nc"]
    in_maps = []
    for b in range(B):
        m = dict(shared)
        m["xT"] = np.ascontiguousarray(x[b].T)
        m["pos"] = np.ascontiguousarray(positions[b:b + 1])
        in_maps.append(m)
    res = run_bass_kernel_spmd(nc, in_maps, core_ids=list(range(B)))
    out = np.stack([np.ascontiguousarray(res.results[b]["yT"].T) for b in range(B)], 0)
    return out.astype(np.float32)
```

```python
import numpy as np
from contextlib import ExitStack
import concourse.bass as bass
import concourse.mybir as mybir
from concourse.bass_utils import run_bass_kernel_spmd

F32 = mybir.dt.float32
BF16 = mybir.dt.bfloat16
I32 = mybir.dt.int32
ALU = mybir.AluOpType
AF = mybir.ActivationFunctionType

D = 1024
DFF = 2816
NJ = DFF // 128
SEQ = 4096
DEPTH = 4
T = 1024
EPS = 1e-6
PI = float(np.pi)
EPOCH = 12000


class Buf:
    __slots__ = ("name", "lw", "rd", "const")

    def __init__(self, name, const=False):
        self.name = name
        self.lw = None
        self.rd = []
        self.const = const


class DmaSem:
    __slots__ = ("name", "count", "handle")

    def __init__(self, name):
        self.name = name
        self.count = 0
        self.handle = None


class Op:
    __slots__ = ("eng", "idx", "fn", "waits", "signaled", "dsem", "dval", "sig_epoch", "sig_val")

    def __init__(self, eng, idx, fn):
        self.eng = eng
        self.idx = idx
        self.fn = fn
        self.waits = []
        self.signaled = False
        self.dsem = None
        self.dval = 0


class Prog:
    ENGS = ("pe", "act", "dve", "pool", "sp")

    def __init__(self):
        self.ops = {e: [] for e in self.ENGS}
        self.known = {e: {} for e in self.ENGS}
        self.dsems = []
        self.bar = {e: [] for e in self.ENGS}
        self.dry = False

    def dma_sem(self, name):
        s = DmaSem(name)
        self.dsems.append(s)
        return s

    def barrier(self, engs=("pe", "act", "dve", "pool")):
        if self.dry:
            return
        last = [self.ops[e][-1] for e in engs if self.ops[e]]
        for e in engs:
            self.bar[e] = list(last)

    def _dep(self, op, d):
        if d is None or d is op:
            return
        if d.dsem is None:
            if d.eng == op.eng:
                return
            key = d.eng
            val = d.idx
        else:
            key = d.dsem
            val = d.dval
        k = self.known[op.eng]
        if k.get(key, -1) >= val:
            return
        k[key] = val
        op.waits.append(d)
        d.signaled = True

    def _same(self, o, d):
        k = self.known[o.eng]
        if k.get(o.eng, -1) < d.idx:
            k[o.eng] = d.idx
            o.waits.append(d)
            d.signaled = True

    def op(self, eng, fn, reads=(), writes=(), dsem=None):
        if self.dry:
            return None
        lst = self.ops[eng]
        o = Op(eng, len(lst), fn)
        if dsem is not None:
            dsem.count += 16
            o.dsem = dsem
            o.dval = dsem.count
        is_async = dsem is not None
        if self.bar[eng]:
            for d in self.bar[eng]:
                self._dep(o, d)
            self.bar[eng] = []
        for b in reads:
            d = b.lw
            if d is not None:
                if d.dsem is None and d.eng == eng:
                    if eng != "pe" or is_async:
                        self._same(o, d)
                else:
                    self._dep(o, d)
        for b in writes:
            d = b.lw
            if d is not None:
                if d.dsem is None and d.eng == eng:
                    if is_async:
                        self._same(o, d)
                else:
                    self._dep(o, d)
            if b.rd:
                best = {}
                for r in b.rd:
                    key = r.eng if r.dsem is None else r.dsem
                    v = r.idx if r.dsem is None else r.dval
                    if key not in best or best[key][0] < v:
                        best[key] = (v, r)
                for v, r in best.values():
                    if r.dsem is None and r.eng == eng:
                        if is_async:
                            self._same(o, r)
                    else:
                        self._dep(o, r)
        for b in writes:
            b.lw = o
            b.rd = []
        for b in reads:
            if b.lw is not o and not b.const:
                b.rd.append(o)
        lst.append(o)
        return o

    def emit(self, nc, es, final_waits=()):
        sems = {}
        for e in self.ENGS:
            cnt = 0
            ep = 0
            for o in self.ops[e]:
                if o.signaled and o.dsem is None:
                    if cnt >= EPOCH:
                        ep += 1
                        cnt = 0
                    cnt += 1
                    o.sig_epoch = ep
                    o.sig_val = cnt
            sems[e] = [es.enter_context(nc.semaphore(f"s_{e}_{i}")) for i in range(ep + 1)]
        for s in self.dsems:
            s.handle = es.enter_context(nc.semaphore(f"d_{s.name}"))
        block = es.enter_context(nc.Block())
        hw = {"pe": "tensor", "act": "scalar", "dve": "vector", "pool": "gpsimd", "sp": "sync"}
        prog = self

        def run(e, eng):
            for o in prog.ops[e]:
                for d in o.waits:
                    if d.dsem is None:
                        eng.wait_ge(sems[d.eng][d.sig_epoch], d.sig_val)
                    else:
                        eng.wait_ge(d.dsem.handle, d.dval)
                ins = o.fn(eng)
                if o.dsem is not None:
                    ins.then_inc(o.dsem.handle, 16)
                elif o.signaled:
                    ins.then_inc(sems[e][o.sig_epoch], 1)
            if e == "sp":
                for d in final_waits:
                    eng.wait_ge(d.dsem.handle, d.dval)

        for e in self.ENGS:
            getattr(block, hw[e])(lambda eng, e=e: run(e, eng))

    def stats(self):
        return {e: (len(self.ops[e]), sum(1 for o in self.ops[e] if o.signaled),
                    sum(len(o.waits) for o in self.ops[e])) for e in self.ENGS}


POOL_W = (2, 4, 8, 16)
GAMMA = [1.0 - 2.0 ** (-5.0 - h) for h in range(4)]
SCL = 128.0 ** -0.5

VP_LAYOUT = {}


def _vp_layout():
    off = 0
    for name, n in (("mixer_g", 32), ("ffn_g", 32), ("final_g", 8), ("ret_g", 8), ("pool_scale", 8),
                    ("b_pw1", 32), ("b_dw", 16), ("ln_g", 16), ("ln_b", 16), ("b_pw2", 16),
                    ("wdw", 2 * 8 * 31), ("invf", 1), ("sign", 1), ("kdec", 4)):
        VP_LAYOUT[name] = off
        off += n
    return off


NV = _vp_layout()


def const_tables():
    m = np.arange(128, dtype=np.float64)
    maskT = np.zeros((128, 4, 128), np.float64)
    qdec = np.zeros((128, 4, 128), np.float64)
    cdt = np.zeros((128, 4, 128), np.float64)
    kdec = np.zeros((128, 4), np.float64)
    for h in range(4):
        g = GAMMA[h]
        maskT[:, h, :] = SCL * (g ** (-m[:, None] - 1.0)) * (m[None, :] >= m[:, None])
        qdec[:, h, :] = (g ** (m + 1.0))[None, :]
        cdt[:, h, :] = g ** 128.0
        kdec[:, h] = SCL * g ** (127.0 - m)
    invcnt = np.zeros((128, 4, 16), np.float64)
    t = np.arange(16)
    for gi, w in enumerate(POOL_W):
        invcnt[:, gi, :] = 1.0 / np.minimum(t + 1, w)[None, :]
    half = 64
    invf = (np.float32(10000.0) ** (-(np.arange(half, dtype=np.float32)) / np.float32(half))).astype(np.float32)
    invf128 = np.concatenate([invf, invf])
    sign = np.concatenate([-np.ones(64), np.ones(64)])
    tabs = np.concatenate([maskT.reshape(128, 512), qdec.reshape(128, 512), cdt.reshape(128, 512),
                           np.eye(128), np.full((128, 128), 1.0 / 128.0), invcnt.reshape(128, 64)], axis=1)
    return tabs.astype(np.float32), invf128.astype(np.float32), sign.astype(np.float32), kdec.astype(np.float32)


NTAB = 512 * 3 + 128 + 128 + 64


def build(seq=SEQ, depth=DEPTH, final_norm=True):
    NP = seq // T
    NS = T // 512
    NCH = T // 128
    nc = bass.Bass("TRN2", target_bir_lowering=False)
    dr = lambda name, shape, dt=F32, kind="ExternalInput": nc.dram_tensor(name, shape, dt, kind=kind).ap()
    xT = dr("xT", [D, seq])
    pos = dr("pos", [1, seq], I32)
    vp_d = dr("vp", [128, NV])
    tab_d = dr("tabs", [128, NTAB])
    wgu_d = dr("wgu", [DEPTH, NJ, 128, 2048])
    wd_d = dr("wd", [DEPTH, 8, 128, NJ * 128])
    win_d = dr("win", [2, 24, 128, 1024])
    winv_d = dr("winv", [2, 128, 4096])
    wout_d = dr("wout", [2, 8, 128, 1024])
    poolw_d = dr("poolw", [2, 128, 512])
    pw1_d = dr("pw1", [2, 16, 128, 1024])
    pw2_d = dr("pw2", [2, 8, 128, 1024])
    yT = dr("yT", [D, seq], F32, "ExternalOutput")

    es = ExitStack()
    P = Prog()
    sbt = lambda name, shape, dt: es.enter_context(nc.sbuf_tensor(name, shape, dt))

    h = sbt("h", [128, 8, T], F32)
    hn = sbt("hn", [128, 8, T], BF16)
    NSLOT = 4
    slots = [sbt(f"slot{i}", [128, 4096], BF16) for i in range(NSLOT)]
    ctab = sbt("ctab", [128, T], F32)
    stab = sbt("stab", [128, T], F32)
    vp = sbt("vp_sb", [128, NV], F32)
    maskT = sbt("maskT", [128, 512], F32)
    qdec = sbt("qdec", [128, 512], F32)
    cdt = sbt("cdt", [128, 512], F32)
    identf = sbt("identf", [128, 128], F32)
    onesf = sbt("onesf", [128, 128], F32)
    invcnt = sbt("invcnt", [128, 64], F32)
    identb = sbt("identb", [128, 128], BF16)
    onesb = sbt("onesb", [128, 128], BF16)
    state = [sbt(f"state{i}", [128, 512], F32) for i in range(2)]
    stbf = [[sbt(f"stbf{i}_{k}", [128, 512], BF16) for k in range(2)] for i in range(2)]
    halo_u = [sbt(f"halo_u{i}", [128, 4, 16], F32) for i in range(2)]
    halo_c = [sbt(f"halo_c{i}", [128, 8, 30], BF16) for i in range(2)]
    rs = sbt("rs", [128, 512], F32)
    rstd = sbt("rstd", [128, 512], F32)
    pwt = sbt("pwt", [128, 512], BF16)
    ARENA = 96256
    arena = sbt("arena", [128, ARENA // 2], BF16)

    def carve(off, shape, dt):
        n = int(np.prod(shape))
        esz = 4 if dt == F32 or dt == I32 else 2
        assert off % 4 == 0 and off + n * esz <= ARENA, (off, shape)
        a = arena[:, off // 2: off // 2 + n * esz // 2]
        if esz == 4:
            a = a.bitcast(dt)
        if len(shape) == 2:
            return a.rearrange("p (a b) -> p a b", a=shape[0])
        if len(shape) == 3:
            return a.rearrange("p (a b c) -> p a b c", a=shape[0], b=shape[1])
        return a

    sq = carve(49152, [8, 512], BF16)
    psf = [es.enter_context(nc.psum_tensor(f"ps{i}", [128, 512], F32)) for i in range(8)]
    ps = [p[:] for p in psf]
    psT = ps[7].bitcast(BF16)

    def mk(name, const=False):
        return Buf(name, const)
    b_h = [[mk(f"h{c}_{s}") for s in range(NS)] for c in range(8)]
    b_hn = [mk(f"hn{s}") for s in range(NS)]
    b_slot = [mk(f"slot{i}") for i in range(NSLOT)]
    d_slot = [P.dma_sem(f"slot{i}") for i in range(NSLOT)]
    b_ps = [mk(f"ps{i}") for i in range(8)]
    b_const = mk("const", True)
    b_ctab = mk("ctab"); b_stab = mk("stab")
    b_state = [mk(f"state{i}") for i in range(2)]
    b_stbf = [[mk(f"stbf{i}_{k}") for k in range(2)] for i in range(2)]
    b_halo_u = [mk(f"halo_u{i}") for i in range(2)]
    b_halo_c = [mk(f"halo_c{i}") for i in range(2)]
    b_rs = mk("rs"); b_rstd = mk("rstd"); b_sq = mk("sq")
    d_x = P.dma_sem("x"); d_c = P.dma_sem("c"); d_o = P.dma_sem("o"); d_pos = P.dma_sem("pos")
    b_pwt = mk("pwt"); d_pwt = P.dma_sem("pwt")

    def vcol(name, i=0):
        o = VP_LAYOUT[name] + i
        return vp[:, o:o + 1]

    P.op("sp", lambda e: e.dma_start(out=vp[:], in_=vp_d), writes=[b_const], dsem=d_c)
    for dst, a, b in ((maskT, 0, 512), (qdec, 512, 1024), (cdt, 1024, 1536), (identf, 1536, 1664),
                      (onesf, 1664, 1792), (invcnt, 1792, 1856)):
        P.op("sp", lambda e, dst=dst, a=a, b=b: e.dma_start(out=dst[:], in_=tab_d[:, a:b]),
             writes=[b_const], dsem=d_c)
    P.op("pool", lambda e: e.memset(onesb[:], 1.0), writes=[b_const])
    P.op("pool", lambda e: e.tensor_copy(out=identb[:], in_=identf[:]), reads=[b_const], writes=[b_const])
    for i in range(2):
        P.op("pool", lambda e, i=i: e.memset(state[i][:], 0.0), writes=[b_state[i]])
        P.op("pool", lambda e, i=i: e.memset(stbf[i][0][:], 0.0), writes=[b_stbf[i][0]])
        P.op("pool", lambda e, i=i: e.memset(halo_u[i][:], 0.0), writes=[b_halo_u[i]])
        P.op("pool", lambda e, i=i: e.memset(halo_c[i][:], 0.0), writes=[b_halo_c[i]])

    slot_ctr = [0]
    plan = []
    issued = [0]
    LOOK = 2

    def load_w(dram_ap, nelem):
        k = slot_ctr[0]
        slot_ctr[0] += 1
        if P.dry:
            plan.append((dram_ap, nelem))
            return k % NSLOT
        while issued[0] < min(len(plan), k + LOOK + 1):
            j = issued[0]
            ap_j, n_j = plan[j]
            i = j % NSLOT
            P.op("pool", lambda e, i=i, ap_j=ap_j, n_j=n_j: e.dma_start(out=slots[i][:, 0:n_j], in_=ap_j),
                 writes=[b_slot[i]], dsem=d_slot[i])
            issued[0] += 1
        return k % NSLOT

    def units(si, n):
        return slots[si][:, 0:n * 1024].rearrange("p (u k n) -> p u k n", u=n, k=8)

    def rmsnorm(gname, layer, out_f32_inplace=False):
        for s in range(NS):
            ts = slice(s * 512, (s + 1) * 512)
            P.op("act", lambda e, ts=ts: e.activation(out=sq, in_=h[:, :, ts], func=AF.Square),
                 reads=[b_h[c][s] for c in range(8)], writes=[b_sq])
            for c in range(8):
                P.op("pe", lambda e, c=c: e.matmul(ps[6], lhsT=onesb[:], rhs=sq[:, c, :],
                                                   start=(c == 0), stop=(c == 7)),
                     reads=[b_const, b_sq], writes=[b_ps[6]])
            P.op("act", lambda e: e.activation(out=rs[:], in_=ps[6], func=AF.Ln, bias=EPS, scale=1.0 / D),
                 reads=[b_ps[6]], writes=[b_rs])
            P.op("act", lambda e: e.activation(out=rstd[:], in_=rs[:], func=AF.Exp, scale=-0.5),
                 reads=[b_rs], writes=[b_rstd])
            for c in range(8):
                gc = vcol(gname, layer * 8 + c)
                if out_f32_inplace:
                    P.op("dve", lambda e, c=c, ts=ts, gc=gc: e.scalar_tensor_tensor(
                        out=h[:, c, ts], in0=h[:, c, ts], scalar=gc, in1=rstd[:], op0=ALU.mult, op1=ALU.mult),
                        reads=[b_h[c][s], b_const, b_rstd], writes=[b_h[c][s]])
                else:
                    P.op("dve", lambda e, c=c, ts=ts, gc=gc: e.scalar_tensor_tensor(
                        out=hn[:, c, ts], in0=h[:, c, ts], scalar=gc, in1=rstd[:], op0=ALU.mult, op1=ALU.mult),
                        reads=[b_h[c][s], b_const, b_rstd], writes=[b_hn[s]])

    def resid_proj(load_fn, src, b_src, nk, bias_name=None, bias_base=0):
        k = 0
        sv = None
        for c in range(8):
            si, sv, uidx = load_fn(c)
            for s in range(NS):
                ts = slice(s * 512, (s + 1) * 512)
                pp = 4 + (k % 2); k += 1
                for j in range(nk):
                    P.op("pe", lambda e, pp=pp, j=j, ts=ts, sv=sv, uidx=uidx: e.matmul(
                        ps[pp], lhsT=sv[:, uidx, j, :], rhs=src[:, j, ts], start=(j == 0), stop=(j == nk - 1)),
                        reads=[b_slot[si], b_src(j, s)], writes=[b_ps[pp]])
                if bias_name is None:
                    P.op("dve", lambda e, pp=pp, c=c, ts=ts: e.tensor_tensor(
                        out=h[:, c, ts], in0=h[:, c, ts], in1=ps[pp], op=ALU.add),
                        reads=[b_h[c][s], b_ps[pp]], writes=[b_h[c][s]])
                else:
                    bc = vcol(bias_name, bias_base + c)
                    P.op("dve", lambda e, pp=pp, c=c, ts=ts, bc=bc: e.scalar_tensor_tensor(
                        out=h[:, c, ts], in0=ps[pp], scalar=bc, in1=h[:, c, ts], op0=ALU.add, op1=ALU.add),
                        reads=[b_h[c][s], b_ps[pp], b_const], writes=[b_h[c][s]])

    def ffn(layer):
        act = carve(0, [NJ, T], BF16)
        sgt = [carve(45056 + i * 2048, [512], F32) for i in range(2)]
        b_act = [[mk(f"act{j}_{s}") for s in range(NS)] for j in range(NJ)]
        b_sg = [mk(f"sg{i}") for i in range(2)]
        rmsnorm("ffn_g", layer)
        k = 0
        for j2 in range(NJ // 2):
            si = load_w(wgu_d[layer, 2 * j2:2 * j2 + 2].rearrange("j p n -> p j n"), 4096)
            sv = slots[si][:].rearrange("p (j g k n) -> p j g k n", j=2, g=2, k=8)
            for jj in range(2):
                j = 2 * j2 + jj
                for s in range(NS):
                    ts = slice(s * 512, (s + 1) * 512)
                    pa = (k % 2) * 2; pb = pa + 1; sgi = k % 2; k += 1
                    for g, pp in ((0, pa), (1, pb)):
                        for kc in range(8):
                            P.op("pe", lambda e, g=g, pp=pp, kc=kc, jj=jj, ts=ts, sv=sv: e.matmul(
                                ps[pp], lhsT=sv[:, jj, g, kc, :], rhs=hn[:, kc, ts],
                                start=(kc == 0), stop=(kc == 7)),
                                reads=[b_slot[si], b_hn[s]], writes=[b_ps[pp]])
                    P.op("act", lambda e, pa=pa, sgi=sgi: e.activation(out=sgt[sgi], in_=ps[pa], func=AF.Silu),
                         reads=[b_ps[pa]], writes=[b_sg[sgi]])
                    P.op("dve", lambda e, pb=pb, sgi=sgi, j=j, ts=ts: e.tensor_tensor(
                        out=act[:, j, ts], in0=sgt[sgi], in1=ps[pb], op=ALU.mult),
                        reads=[b_sg[sgi], b_ps[pb]], writes=[b_act[j][s]])

        def ld(c):
            si = load_w(wd_d[layer, c], NJ * 128)
            sv = slots[si][:, 0:NJ * 128].rearrange("p (u j n) -> p u j n", u=1, j=NJ)
            return si, sv, 0
        resid_proj(ld, act, lambda j, s: b_act[j][s], NJ)

    def rope_tables(p0):
        pos_i = carve(0, [T], I32)
        t0 = carve(4096, [T], F32)
        yv = carve(8192, [T], F32)
        n_i = carve(12288, [T], I32)
        r = carve(16384, [T], F32)
        m = carve(20480, [T], F32)
        rc = carve(24576, [T], F32)
        b = {n: mk("rt_" + n) for n in ("pos", "t0", "y", "ni", "r", "m", "rc")}
        P.op("sp", lambda e: e.dma_start(out=pos_i, in_=bass.AP(pos.tensor, p0, [[0, 128], [1, T]])),
             writes=[b["pos"]], dsem=d_pos)
        P.op("dve", lambda e: e.tensor_copy(out=t0, in_=pos_i), reads=[b["pos"]], writes=[b["t0"]])
        P.op("dve", lambda e: e.tensor_scalar(out=t0, in0=t0, scalar1=vcol("invf"), scalar2=None, op0=ALU.mult),
             reads=[b["t0"], b_const], writes=[b["t0"]])
        P.op("dve", lambda e: e.tensor_scalar(out=yv, in0=t0, scalar1=1.0 / (2 * PI), scalar2=None, op0=ALU.mult),
             reads=[b["t0"]], writes=[b["y"]])
        P.op("dve", lambda e: e.tensor_copy(out=n_i, in_=yv), reads=[b["y"]], writes=[b["ni"]])
        P.op("dve", lambda e: e.tensor_copy(out=yv, in_=n_i), reads=[b["ni"]], writes=[b["y"]])
        C1 = 6.28125
        C2 = float(2 * np.pi - 6.28125)
        P.op("dve", lambda e: e.scalar_tensor_tensor(out=r, in0=yv, scalar=-C1, in1=t0, op0=ALU.mult, op1=ALU.add),
             reads=[b["y"], b["t0"]], writes=[b["r"]])
        P.op("dve", lambda e: e.scalar_tensor_tensor(out=r, in0=yv, scalar=-C2, in1=r, op0=ALU.mult, op1=ALU.add),
             reads=[b["y"], b["r"]], writes=[b["r"]])

        def wrap(x, bx):
            P.op("dve", lambda e: e.tensor_scalar(out=m, in0=x, scalar1=PI, scalar2=-2 * PI, op0=ALU.is_gt, op1=ALU.mult),
                 reads=[bx], writes=[b["m"]])
            P.op("dve", lambda e: e.tensor_tensor(out=x, in0=x, in1=m, op=ALU.add), reads=[bx, b["m"]], writes=[bx])
            P.op("dve", lambda e: e.tensor_scalar(out=x, in0=x, scalar1=PI, scalar2=-PI, op0=ALU.min, op1=ALU.max),
                 reads=[bx], writes=[bx])
        wrap(r, b["r"])
        P.op("dve", lambda e: e.tensor_scalar(out=rc, in0=r, scalar1=PI / 2, scalar2=None, op0=ALU.add),
             reads=[b["r"]], writes=[b["rc"]])
        wrap(rc, b["rc"])
        P.op("act", lambda e: e.activation(out=stab[:], in_=r, func=AF.Sin, scale=vcol("sign")),
             reads=[b["r"], b_const], writes=[b_stab])
        P.op("act", lambda e: e.activation(out=ctab[:], in_=rc, func=AF.Sin), reads=[b["rc"]], writes=[b_ctab])

    def even_mixer(layer, pidx):
        i2 = layer // 2
        qr = carve(0, [4, T], BF16)
        kr = carve(8192, [4, T], BF16)
        v_tok = carve(16384, [NCH, 512], BF16)
        sgn = carve(24576, [NCH, 4, 128], BF16)
        mix = carve(32768, [8, T], BF16)
        rt1 = [carve(49152 + i * 2048, [512], F32) for i in range(2)]
        rt2 = [carve(53248 + i * 2048, [512], F32) for i in range(2)]
        gt = [carve(57344 + i * 2048, [512], F32) for i in range(2)]
        ub = [carve(61440 + i * 4160, [16 + T], F32) for i in range(4)]
        pt = [carve(78080 + i * 4160, [16 + T], F32) for i in range(2)]
        yb = [carve(86400 + i * 2048, [T], BF16) for i in range(4)]
        kd = [carve(49152 + i * 1024, [512], BF16) for i in range(2)]
        Sm = [carve(51200 + i * 1024, [512], BF16) for i in range(2)]
        tmpB = [[carve(53248 + (i * 7 + k) * 2048, [512], F32) for k in range(7)] for i in range(2)]

        b_qr = [[mk(f"qr{hh}_{s}") for s in range(NS)] for hh in range(4)]
        b_kr = [[mk(f"kr{hh}_{s}") for s in range(NS)] for hh in range(4)]
        b_v = [mk(f"v{n}") for n in range(NCH)]
        b_sgn = [[mk(f"sgn{hh}_{s}") for s in range(NS)] for hh in range(4)]
        b_mix = [[mk(f"mix{j}_{s}") for s in range(NS)] for j in range(8)]
        b_rt1 = [mk("rt1a"), mk("rt1b")]; b_rt2 = [mk("rt2a"), mk("rt2b")]
        b_ub = [mk(f"ub{i}") for i in range(4)]; b_pt = [mk("pt0"), mk("pt1")]; b_yb = [mk(f"yb{i}") for i in range(4)]
        b_gt = [mk("gt0"), mk("gt1")]
        b_kd = [mk("kd0"), mk("kd1")]
        b_Sm = [mk("Sm0"), mk("Sm1")]
        tn_ = ("o_sb", "osq", "msq", "var", "sd", "rinv", "xc")
        b_t = [{n: mk(f"t{i}_" + n) for n in tn_} for i in range(2)]

        rmsnorm("mixer_g", layer)
        P.barrier()
        P.op("pool", lambda e: e.dma_start(out=pwt[:], in_=poolw_d[i2]), writes=[b_pwt], dsem=d_pwt)
        pw_v = pwt[:].rearrange("p (g n) -> p g n", g=4)
        kk = [0]
        cur_slot = [0]

        def proj_pair(sv, ua, ub_, s):
            ts = slice(s * 512, (s + 1) * 512)
            pa = (kk[0] % 2) * 2; pb = pa + 1; kk[0] += 1
            for u, pp in ((ua, pa), (ub_, pb)):
                if u is None:
                    continue
                for kc in range(8):
                    P.op("pe", lambda e, u=u, pp=pp, kc=kc, ts=ts, sv=sv: e.matmul(
                        ps[pp], lhsT=sv[:, u, kc, :], rhs=hn[:, kc, ts], start=(kc == 0), stop=(kc == 7)),
                        reads=[b_slot[cur_slot[0]], b_hn[s]], writes=[b_ps[pp]])
            return pa, pb

        si_u = load_w(win_d[i2, 20:24].rearrange("u p n -> p u n"), 4096)
        svu = units(si_u, 4)
        cur_slot[0] = si_u
        for g in range(4):
            U = ub[g]
            P.op("pool", lambda e, U=U, g=g: e.tensor_copy(out=U[:, 0:16], in_=halo_u[i2][:, g, :]),
                 reads=[b_halo_u[i2]], writes=[b_ub[g]])
            for s in range(NS):
                pa, _ = proj_pair(svu, g, None, s)
                P.op("act", lambda e, pa=pa, U=U, s=s: e.activation(
                    out=U[:, 16 + s * 512:16 + (s + 1) * 512], in_=ps[pa], func=AF.Copy),
                    reads=[b_ps[pa]], writes=[b_ub[g]])
            P.op("pool", lambda e, U=U, g=g: e.tensor_copy(out=halo_u[i2][:, g, :], in_=U[:, T:T + 16]),
                 reads=[b_ub[g]], writes=[b_halo_u[i2]])

        def pool_chain(g):
            w = POOL_W[g]
            U = ub[g]
            src, bsrc = U, b_ub[g]
            sh = 1; lo = 0; k = 0
            while sh < w:
                dst, bd = pt[k % 2], b_pt[k % 2]
                lo2 = lo + sh
                P.op("dve", lambda e, src=src, dst=dst, lo2=lo2, sh=sh: e.tensor_tensor(
                    out=dst[:, lo2:16 + T], in0=src[:, lo2:16 + T], in1=src[:, lo2 - sh:16 + T - sh], op=ALU.add),
                    reads=[bsrc], writes=[bd])
                src, bsrc = dst, bd
                lo = lo2; sh *= 2; k += 1
            Y = yb[g]
            P.op("dve", lambda e, src=src, U=U, Y=Y, w=w: e.scalar_tensor_tensor(
                out=Y, in0=src[:, 16:16 + T], scalar=1.0 / w, in1=U[:, 16:16 + T], op0=ALU.mult, op1=ALU.subtract),
                reads=[bsrc, b_ub[g]], writes=[b_yb[g]])
            if pidx == 0:
                P.op("dve", lambda e, src=src, g=g: e.tensor_tensor(
                    out=src[:, 16:32], in0=src[:, 16:32], in1=invcnt[:, g * 16:(g + 1) * 16], op=ALU.mult),
                    reads=[bsrc, b_const, b_yb[g]], writes=[bsrc])
                P.op("dve", lambda e, src=src, U=U, Y=Y: e.tensor_tensor(
                    out=Y[:, 0:16], in0=src[:, 16:32], in1=U[:, 16:32], op=ALU.subtract),
                    reads=[bsrc, b_ub[g]], writes=[b_yb[g]])
        for g in range(4):
            pool_chain(g)

        ri = 0
        for x, dst, bdst in ((0, qr, b_qr), (1, kr, b_kr)):
            for hp in range(2):
                si = load_w(win_d[i2, (x * 2 + hp) * 4:(x * 2 + hp) * 4 + 4].rearrange("u p n -> p u n"), 4096)
                cur_slot[0] = si
                sv = units(si, 4)
                for hl in range(2):
                    hh = hp * 2 + hl
                    for s in range(NS):
                        ts = slice(s * 512, (s + 1) * 512)
                        pa, pb = proj_pair(sv, hl * 2, hl * 2 + 1, s)
                        r1 = ri % 2; ri += 1
                        P.op("dve", lambda e, pa=pa, r1=r1, ts=ts: e.tensor_tensor(
                            out=rt1[r1], in0=ps[pa], in1=ctab[:, ts], op=ALU.mult),
                            reads=[b_ps[pa], b_ctab], writes=[b_rt1[r1]])
                        P.op("dve", lambda e, pb=pb, r1=r1, ts=ts: e.tensor_tensor(
                            out=rt2[r1], in0=ps[pb], in1=stab[:, ts], op=ALU.mult),
                            reads=[b_ps[pb], b_stab], writes=[b_rt2[r1]])
                        P.op("pool", lambda e, r1=r1, dst=dst, hh=hh, ts=ts: e.tensor_tensor(
                            out=dst[:, hh, ts], in0=rt1[r1], in1=rt2[r1], op=ALU.add),
                            reads=[b_rt1[r1], b_rt2[r1]], writes=[bdst[hh][s]])
        si = load_w(win_d[i2, 16:20].rearrange("u p n -> p u n"), 4096)
        cur_slot[0] = si
        sv = units(si, 4)
        gi_ = 0
        for hh in range(4):
            for s in range(NS):
                pa, _ = proj_pair(sv, hh, None, s)
                g1 = gi_ % 2; gi_ += 1
                P.op("act", lambda e, pa=pa, g1=g1: e.activation(out=gt[g1], in_=ps[pa], func=AF.Silu),
                     reads=[b_ps[pa]], writes=[b_gt[g1]])
                P.op("dve", lambda e, g1=g1, hh=hh, s=s: e.tensor_scalar(
                    out=sgn[:, s * 4:(s + 1) * 4, hh, :], in0=gt[g1].rearrange("p (n c) -> p n c", n=4),
                    scalar1=vcol("ret_g", i2 * 4 + hh), scalar2=None, op0=ALU.mult),
                    reads=[b_gt[g1], b_const], writes=[b_sgn[hh][s]])
        si = load_w(winv_d[i2], 4096)
        svv = slots[si][:].rearrange("p (k n) -> p k n", k=8)
        for n in range(NCH):
            pp = 4 + (n % 2)
            tn = slice(n * 128, (n + 1) * 128)
            for kc in range(8):
                P.op("pe", lambda e, pp=pp, kc=kc, tn=tn, svv=svv: e.matmul(
                    ps[pp], lhsT=hn[:, kc, tn], rhs=svv[:, kc, :], start=(kc == 0), stop=(kc == 7)),
                    reads=[b_slot[si], b_hn[n // 4]], writes=[b_ps[pp]])
            P.op("act", lambda e, pp=pp, n=n: e.activation(out=v_tok[:, n, :], in_=ps[pp], func=AF.Copy),
                 reads=[b_ps[pp]], writes=[b_v[n]])
        for g in range(4):
            for s in range(NS):
                ts = slice(s * 512, (s + 1) * 512)
                pp = 4 + (s % 2)
                P.op("pe", lambda e, pp=pp, g=g, ts=ts: e.matmul(
                    ps[pp], lhsT=pw_v[:, g, :], rhs=yb[g][:, ts], start=True, stop=True),
                    reads=[b_pwt, b_yb[g]], writes=[b_ps[pp]])
                P.op("act", lambda e, pp=pp, g=g, ts=ts: e.activation(
                    out=mix[:, 4 + g, ts], in_=ps[pp], func=AF.Identity, scale=vcol("pool_scale", i2 * 4 + g)),
                    reads=[b_ps[pp], b_const], writes=[b_mix[4 + g][s]])
        P.barrier()
        psKV, psS, psO, psM, psE = 0, (1, 2), (3, 4), 5, 6
        st, bst = state[i2], b_state[i2]

        def stage1(n):
            s = n // 4
            tn = slice(n * 128, (n + 1) * 128)
            i = n % 2
            for hh in range(4):
                P.op("pe", lambda e, hh=hh, tn=tn: e.transpose(
                    out=psT[:, hh * 128:(hh + 1) * 128], in_=kr[:, hh, tn], identity=identb[:]),
                    reads=[b_kr[hh][s], b_const], writes=[b_ps[7]])
            for hh in range(4):
                P.op("act", lambda e, hh=hh, i=i: e.activation(
                    out=kd[i][:, hh * 128:(hh + 1) * 128], in_=psT[:, hh * 128:(hh + 1) * 128],
                    func=AF.Identity, scale=vcol("kdec", hh)), reads=[b_ps[7], b_const], writes=[b_kd[i]])
            pS = psS[i]
            for hh in range(4):
                P.op("pe", lambda e, hh=hh, tn=tn, pS=pS: e.matmul(
                    ps[pS][:, hh * 128:(hh + 1) * 128], lhsT=kr[:, hh, tn], rhs=qr[:, hh, tn], start=True, stop=True),
                    reads=[b_kr[hh][s], b_qr[hh][s]], writes=[b_ps[pS]])
            P.op("dve", lambda e, pS=pS, i=i: e.tensor_tensor(out=Sm[i], in0=ps[pS], in1=maskT[:], op=ALU.mult),
                 reads=[b_ps[pS], b_const], writes=[b_Sm[i]])

        def stage2(n):
            s = n // 4
            tn = slice(n * 128, (n + 1) * 128)
            i = n % 2
            cb = n % 2
            pO = psO[i]
            o_sb, osq = tmpB[i][0], tmpB[i][1]
            for hh in range(4):
                hs = slice(hh * 128, (hh + 1) * 128)
                P.op("pe", lambda e, hs=hs, n=n, pO=pO, i=i: e.matmul(
                    ps[pO][:, hs], lhsT=v_tok[:, n, hs], rhs=Sm[i][:, hs], start=True, stop=False),
                    reads=[b_v[n], b_Sm[i]], writes=[b_ps[pO]])
                P.op("pe", lambda e, hs=hs, hh=hh, tn=tn, pO=pO, cb=cb: e.matmul(
                    ps[pO][:, hs], lhsT=stbf[i2][cb][:, hs], rhs=qr[:, hh, tn], start=False, stop=True),
                    reads=[b_stbf[i2][cb], b_qr[hh][s]], writes=[b_ps[pO]])
            for hh in range(4):
                hs = slice(hh * 128, (hh + 1) * 128)
                P.op("pe", lambda e, hs=hs, n=n, i=i: e.matmul(
                    ps[psKV][:, hs], lhsT=kd[i][:, hs], rhs=v_tok[:, n, hs], start=True, stop=True),
                    reads=[b_kd[i], b_v[n]], writes=[b_ps[psKV]])
            for hh in range(4):
                hs = slice(hh * 128, (hh + 1) * 128)
                P.op("dve", lambda e, hs=hs, hh=hh: e.scalar_tensor_tensor(
                    out=st[:, hs], in0=st[:, hs], scalar=float(GAMMA[hh] ** 128.0), in1=ps[psKV][:, hs],
                    op0=ALU.mult, op1=ALU.add), reads=[bst, b_ps[psKV]], writes=[bst])
            P.op("act", lambda e, cb=cb: e.activation(out=stbf[i2][1 - cb][:], in_=st[:], func=AF.Copy),
                 reads=[bst], writes=[b_stbf[i2][1 - cb]])
            P.op("dve", lambda e, pO=pO, o_sb=o_sb: e.tensor_tensor(out=o_sb, in0=ps[pO], in1=qdec[:], op=ALU.mult),
                 reads=[b_ps[pO], b_const], writes=[b_t[i]["o_sb"]])
            P.op("act", lambda e, o_sb=o_sb, osq=osq: e.activation(out=osq, in_=o_sb, func=AF.Square),
                 reads=[b_t[i]["o_sb"]], writes=[b_t[i]["osq"]])

        def stage3(n):
            s = n // 4
            tn = slice(n * 128, (n + 1) * 128)
            i = n % 2
            o_sb, osq, msq, var, sd, rinv, xc = tmpB[i]
            bt = b_t[i]
            P.op("pe", lambda e: e.matmul(ps[psM], lhsT=onesf[:], rhs=o_sb, start=True, stop=True),
                 reads=[b_const, bt["o_sb"]], writes=[b_ps[psM]])
            P.op("pe", lambda e: e.matmul(ps[psE], lhsT=onesf[:], rhs=osq, start=True, stop=True),
                 reads=[b_const, bt["osq"]], writes=[b_ps[psE]])
            P.op("act", lambda e: e.activation(out=msq, in_=ps[psM], func=AF.Square),
                 reads=[b_ps[psM]], writes=[bt["msq"]])
            P.op("dve", lambda e: e.tensor_tensor(out=var, in0=ps[psE], in1=msq, op=ALU.subtract),
                 reads=[b_ps[psE], bt["msq"]], writes=[bt["var"]])
            P.op("act", lambda e: e.activation(out=sd, in_=var, func=AF.Ln, bias=EPS, scale=1.0),
                 reads=[bt["var"]], writes=[bt["sd"]])
            P.op("act", lambda e: e.activation(out=rinv, in_=sd, func=AF.Exp, scale=-0.5),
                 reads=[bt["sd"]], writes=[bt["rinv"]])
            P.op("dve", lambda e: e.tensor_tensor(out=xc, in0=o_sb, in1=ps[psM], op=ALU.subtract),
                 reads=[bt["o_sb"], b_ps[psM]], writes=[bt["xc"]])
            P.op("pool", lambda e: e.tensor_tensor(out=xc, in0=xc, in1=rinv, op=ALU.mult),
                 reads=[bt["xc"], bt["rinv"]], writes=[bt["xc"]])
            P.op("pool", lambda e, n=n, tn=tn: e.tensor_tensor(
                out=mix[:, 0:4, tn], in0=xc.rearrange("p (h c) -> p h c", h=4), in1=sgn[:, n, :, :], op=ALU.mult),
                reads=[bt["xc"]] + [b_sgn[hh][s] for hh in range(4)],
                writes=[b_mix[hh][s] for hh in range(4)])

        for it in range(NCH + 2):
            if it < NCH:
                stage1(it)
            if 0 <= it - 1 < NCH:
                stage2(it - 1)
            if 0 <= it - 2 < NCH:
                stage3(it - 2)

        def ld(c):
            if c % 4 == 0:
                ld.si = load_w(wout_d[i2, c:c + 4].rearrange("u p n -> p u n"), 4096)
                ld.sv = units(ld.si, 4)
            return ld.si, ld.sv, c % 4
        resid_proj(ld, mix, lambda j, s: b_mix[j][s], 8)

    def odd_mixer(layer, pidx):
        i2 = layer // 2
        U = carve(0, [8, 30 + T], BF16)
        dg = [carve(16864 + i * 7936, [31, 128], BF16) for i in range(3)]
        dw = carve(40672, [8, T], F32)
        sig = [carve(73440 + i * 2048, [512], F32) for i in range(2)]
        dwb = [carve(77536 + i * 1024, [512], BF16) for i in range(2)]
        sqb = [carve(79584 + i * 1024, [512], BF16) for i in range(2)]
        mean, msq, var, sd, rinv = [carve(81632 + i * 2048, [512], F32) for i in range(5)]
        b_U = [[mk(f"U{c}_{s}") for s in range(NS)] for c in range(8)]
        b_Uh = [mk(f"Uh{c}") for c in range(8)]
        b_dg = [mk("dg0"), mk("dg1"), mk("dg2")]
        b_dw = [[mk(f"dw{c}_{s}") for s in range(NS)] for c in range(8)]
        b_sig = [mk("sig0"), mk("sig1")]
        b_dwb = [mk("dwb0"), mk("dwb1")]; b_sqb = [mk("sqb0"), mk("sqb1")]
        b_t = {n: mk("ot_" + n) for n in ("mean", "msq", "var", "sd", "rinv")}

        rmsnorm("mixer_g", layer)
        P.barrier()
        wd0 = VP_LAYOUT["wdw"] + i2 * 8 * 31

        def build_dg(c):
            d1 = c % 3
            wsl = vp[:, wd0 + c * 31: wd0 + (c + 1) * 31]
            P.op("pool", lambda e, d1=d1, wsl=wsl: e.tensor_tensor(
                out=dg[d1], in0=identf[:].unsqueeze(1).broadcast_to([128, 31, 128]),
                in1=wsl.unsqueeze(2).broadcast_to([128, 31, 128]), op=ALU.mult),
                reads=[b_const], writes=[b_dg[d1]])
        for c in range(8):
            P.op("pool", lambda e, c=c: e.tensor_copy(out=U[:, c, 0:30], in_=halo_c[i2][:, c, :]),
                 reads=[b_halo_c[i2]], writes=[b_Uh[c]])
        for c in range(3):
            build_dg(c)
        kk = 0
        for c in range(8):
            if c % 2 == 0:
                si = load_w(pw1_d[i2, 2 * c:2 * c + 4].rearrange("u p n -> p u n"), 4096)
                sv = units(si, 4)
            for s in range(NS):
                ts = slice(s * 512, (s + 1) * 512)
                pa = (kk % 2) * 2; pb = pa + 1; sg1 = kk % 2; kk += 1
                for u, pp in (((c % 2) * 2, pa), ((c % 2) * 2 + 1, pb)):
                    for kc in range(8):
                        P.op("pe", lambda e, u=u, pp=pp, kc=kc, ts=ts, sv=sv: e.matmul(
                            ps[pp], lhsT=sv[:, u, kc, :], rhs=hn[:, kc, ts], start=(kc == 0), stop=(kc == 7)),
                            reads=[b_slot[si], b_hn[s]], writes=[b_ps[pp]])
                P.op("act", lambda e, pb=pb, sg1=sg1, c=c: e.activation(
                    out=sig[sg1], in_=ps[pb], func=AF.Sigmoid, bias=vcol("b_pw1", i2 * 16 + 8 + c), scale=1.0),
                    reads=[b_ps[pb], b_const], writes=[b_sig[sg1]])
                P.op("dve", lambda e, pa=pa, sg1=sg1, c=c, s=s: e.scalar_tensor_tensor(
                    out=U[:, c, 30 + s * 512:30 + (s + 1) * 512], in0=ps[pa], scalar=vcol("b_pw1", i2 * 16 + c),
                    in1=sig[sg1], op0=ALU.add, op1=ALU.mult),
                    reads=[b_ps[pa], b_const, b_sig[sg1]], writes=[b_U[c][s]])
        for c in range(8):
            P.op("pool", lambda e, c=c: e.tensor_copy(out=halo_c[i2][:, c, :], in_=U[:, c, T:T + 30]),
                 reads=[b_U[c][NS - 1]], writes=[b_halo_c[i2]])
        psC, psM, psE = (0, 1), (2, 3), (4, 5)
        kk = 0
        pend = None

        def stats(c, s, k1):
            P.op("pe", lambda e: e.matmul(ps[psM[s]], lhsT=onesb[:], rhs=dwb[k1], start=(c == 0), stop=(c == 7)),
                 reads=[b_const, b_dwb[k1]], writes=[b_ps[psM[s]]])
            P.op("pe", lambda e: e.matmul(ps[psE[s]], lhsT=onesb[:], rhs=sqb[k1], start=(c == 0), stop=(c == 7)),
                 reads=[b_const, b_sqb[k1]], writes=[b_ps[psE[s]]])
        for c in range(8):
            d1 = c % 3
            for s in range(NS):
                ts = slice(s * 512, (s + 1) * 512)
                pc = psC[kk % 2]; k1 = kk % 2; kk += 1
                rd = [b_dg[d1], b_U[c][s], b_Uh[c]] + ([b_U[c][s - 1]] if s > 0 else [])
                for j in range(31):
                    P.op("pe", lambda e, pc=pc, d1=d1, j=j, c=c, s=s: e.matmul(
                        ps[pc], lhsT=dg[d1][:, j, :], rhs=U[:, c, s * 512 + j:s * 512 + j + 512],
                        start=(j == 0), stop=(j == 30)), reads=rd, writes=[b_ps[pc]])
                if pend is not None:
                    stats(*pend)
                bcol = vcol("b_dw", i2 * 8 + c)
                P.op("act", lambda e, pc=pc, c=c, ts=ts, bcol=bcol: e.activation(
                    out=dw[:, c, ts], in_=ps[pc], func=AF.Identity, bias=bcol, scale=1.0),
                    reads=[b_ps[pc], b_const], writes=[b_dw[c][s]])
                P.op("act", lambda e, pc=pc, k1=k1, bcol=bcol: e.activation(
                    out=dwb[k1], in_=ps[pc], func=AF.Identity, bias=bcol, scale=1.0),
                    reads=[b_ps[pc], b_const], writes=[b_dwb[k1]])
                P.op("act", lambda e, pc=pc, k1=k1, bcol=bcol: e.activation(
                    out=sqb[k1], in_=ps[pc], func=AF.Square, bias=bcol, scale=1.0),
                    reads=[b_ps[pc], b_const], writes=[b_sqb[k1]])
                pend = (c, s, k1)
            if c + 3 < 8:
                build_dg(c + 3)
        stats(*pend)
        si_a = load_w(pw2_d[i2, 0:4].rearrange("u p n -> p u n"), 4096)
        si_b = load_w(pw2_d[i2, 4:8].rearrange("u p n -> p u n"), 4096)
        sv_ab = (units(si_a, 4), units(si_b, 4))
        si_ab = (si_a, si_b)
        kq = 0
        for s in range(NS):
            ts = slice(s * 512, (s + 1) * 512)
            P.op("act", lambda e, s=s: e.activation(out=mean, in_=ps[psM[s]], func=AF.Copy, scale=1.0 / D),
                 reads=[b_ps[psM[s]]], writes=[b_t["mean"]])
            P.op("act", lambda e, s=s: e.activation(out=msq, in_=ps[psM[s]], func=AF.Square, scale=1.0 / D),
                 reads=[b_ps[psM[s]]], writes=[b_t["msq"]])
            P.op("dve", lambda e, s=s: e.scalar_tensor_tensor(
                out=var, in0=ps[psE[s]], scalar=1.0 / D, in1=msq, op0=ALU.mult, op1=ALU.subtract),
                reads=[b_ps[psE[s]], b_t["msq"]], writes=[b_t["var"]])
            P.op("act", lambda e: e.activation(out=sd, in_=var, func=AF.Ln, bias=EPS, scale=1.0),
                 reads=[b_t["var"]], writes=[b_t["sd"]])
            P.op("act", lambda e: e.activation(out=rinv, in_=sd, func=AF.Exp, scale=-0.5),
                 reads=[b_t["sd"]], writes=[b_t["rinv"]])
            for c in range(8):
                P.op("dve", lambda e, c=c, ts=ts: e.tensor_tensor(out=dw[:, c, ts], in0=dw[:, c, ts], in1=mean, op=ALU.subtract),
                     reads=[b_dw[c][s], b_t["mean"]], writes=[b_dw[c][s]])
                eng2 = "pool" if c in (2, 5) else "dve"
                P.op(eng2, lambda e, c=c, ts=ts: e.tensor_tensor(out=dw[:, c, ts], in0=dw[:, c, ts], in1=rinv, op=ALU.mult),
                     reads=[b_dw[c][s], b_t["rinv"]], writes=[b_dw[c][s]])
                P.op("act", lambda e, c=c, ts=ts: e.activation(
                    out=hn[:, c, ts], in_=dw[:, c, ts], func=AF.Silu,
                    bias=vcol("ln_b", i2 * 8 + c), scale=vcol("ln_g", i2 * 8 + c)),
                    reads=[b_dw[c][s], b_const], writes=[b_hn[s]])
            for c in range(8):
                pp = 6 + (kq % 2); kq += 1
                sv = sv_ab[c // 4]
                for j in range(8):
                    P.op("pe", lambda e, pp=pp, j=j, ts=ts, sv=sv, c=c: e.matmul(
                        ps[pp], lhsT=sv[:, c % 4, j, :], rhs=hn[:, j, ts], start=(j == 0), stop=(j == 7)),
                        reads=[b_slot[si_ab[c // 4]], b_hn[s]], writes=[b_ps[pp]])
                bc = vcol("b_pw2", i2 * 8 + c)
                P.op("dve", lambda e, pp=pp, c=c, ts=ts, bc=bc: e.scalar_tensor_tensor(
                    out=h[:, c, ts], in0=ps[pp], scalar=bc, in1=h[:, c, ts], op0=ALU.add, op1=ALU.add),
                    reads=[b_h[c][s], b_ps[pp], b_const], writes=[b_h[c][s]])

    xv = xT.rearrange("(c p) t -> p c t", p=128)
    yv = yT.rearrange("(c p) t -> p c t", p=128)
    fin = []

    def schedule():
        for pidx in range(NP):
            p0 = pidx * T
            for c in range(8):
                P.op("sp", lambda e, c=c, p0=p0: e.dma_start(out=h[:, c, :], in_=xv[:, c, p0:p0 + T]),
                     writes=[b_h[c][s] for s in range(NS)], dsem=d_x)
            P.barrier()
            rope_tables(p0)
            for layer in range(depth):
                P.barrier()
                if layer % 2 == 0:
                    even_mixer(layer, pidx)
                else:
                    odd_mixer(layer, pidx)
                P.barrier()
                ffn(layer)
            if final_norm:
                rmsnorm("final_g", 0, out_f32_inplace=True)
            for c in range(8):
                o = P.op("sp", lambda e, c=c, p0=p0: e.dma_start(out=yv[:, c, p0:p0 + T], in_=h[:, c, :]),
                         reads=[b_h[c][s] for s in range(NS)], dsem=d_o)
                if o is not None:
                    fin.append(o)

    P.dry = True
    schedule()
    P.dry = False
    slot_ctr[0] = 0
    schedule()
    P.emit(nc, es, final_waits=fin[-8:])
    es.close()
    return nc, P


def _units(w):
    n = w.shape[1] // 128
    return np.ascontiguousarray(w.reshape(8, 128, n, 128).transpose(2, 1, 0, 3)).reshape(n, 128, 1024)


def prep_weights(inp):
    f = lambda a: np.asarray(a, dtype=np.float32)
    out = {}
    wg = f(inp["ffn_w_gate"]); wu = f(inp["ffn_w_up"]); wdn = f(inp["ffn_w_down"])
    L = wg.shape[0]
    wgu = np.stack([wg.reshape(L, 8, 128, NJ, 128), wu.reshape(L, 8, 128, NJ, 128)], 1)
    out["wgu"] = np.ascontiguousarray(wgu.transpose(0, 4, 3, 1, 2, 5)).reshape(L, NJ, 128, 2048)
    out["wd"] = np.ascontiguousarray(wdn.reshape(L, NJ, 128, 8, 128).transpose(0, 3, 2, 1, 4)).reshape(L, 8, 128, NJ * 128)
    w_in = f(inp["ret_w_in"])
    perm = np.concatenate([np.arange(64, 128), np.arange(0, 64)])
    win = []
    winv = []
    for i in range(w_in.shape[0]):
        W = w_in[i]
        us = {}
        for x, base in (("q", 0), ("k", 512)):
            blk = _units(W[:, base:base + 512])
            rot = _units(W[:, base:base + 512].reshape(D, 4, 128)[:, :, perm].reshape(D, 512))
            us[x] = (blk, rot)
        order = []
        for x in ("q", "k"):
            blk, rot = us[x]
            for hp in range(2):
                for hl in range(2):
                    hh = hp * 2 + hl
                    order += [blk[hh], rot[hh]]
        g_u = _units(W[:, 1536:2048])
        u_u = _units(W[:, 2048:2560])
        order += [g_u[k] for k in range(4)] + [u_u[k] for k in range(4)]
        win.append(np.stack(order, 0))
        winv.append(np.ascontiguousarray(W[:, 1024:1536].reshape(8, 128, 512).transpose(1, 0, 2)).reshape(128, 4096))
    out["win"] = np.stack(win, 0)
    out["winv"] = np.stack(winv, 0)
    wo = f(inp["mix_w_out"])
    out["wout"] = np.stack([_units(wo[i]) for i in range(wo.shape[0])], 0)
    pw = f(inp["pool_w"])
    out["poolw"] = np.ascontiguousarray(pw.transpose(0, 2, 1, 3)).reshape(pw.shape[0], 128, 512)
    w1 = f(inp["conv_w_pw1"])
    p1 = []
    for i in range(w1.shape[0]):
        a_u = _units(w1[i][:, :D]); g_u = _units(w1[i][:, D:])
        o = []
        for c in range(8):
            o += [a_u[c], g_u[c]]
        p1.append(np.stack(o, 0))
    out["pw1"] = np.stack(p1, 0)
    w2 = f(inp["conv_w_pw2"])
    out["pw2"] = np.stack([_units(w2[i]) for i in range(w2.shape[0])], 0)
    vp = np.zeros((128, NV), np.float32)

    def put(name, arr, base=0):
        arr = np.asarray(arr, np.float32)
        n = arr.shape[0] // 128
        vp[:, VP_LAYOUT[name] + base: VP_LAYOUT[name] + base + n] = arr.reshape(n, 128).T
    for l in range(DEPTH):
        put("mixer_g", f(inp["mixer_norm_g"])[l], l * 8)
        put("ffn_g", f(inp["ffn_norm_g"])[l], l * 8)
    put("final_g", f(inp["final_norm_g"]))
    for i in range(2):
        put("ret_g", f(inp["ret_norm_g"])[i], i * 4)
        put("pool_scale", f(inp["pool_scale"])[i], i * 4)
        put("b_pw1", f(inp["conv_b_pw1"])[i], i * 16)
        put("b_dw", f(inp["conv_b_dw"])[i], i * 8)
        put("ln_g", f(inp["conv_ln_g"])[i], i * 8)
        put("ln_b", f(inp["conv_ln_b"])[i], i * 8)
        put("b_pw2", f(inp["conv_b_pw2"])[i], i * 8)
        wdw = f(inp["conv_w_dw"])[i]
        o = VP_LAYOUT["wdw"] + i * 8 * 31
        vp[:, o:o + 248] = wdw.reshape(31, 8, 128).transpose(2, 1, 0).reshape(128, 248)
    tabs, invf, sign, kdec = const_tables()
    vp[:, VP_LAYOUT["invf"]] = invf
    vp[:, VP_LAYOUT["sign"]] = sign
    vp[:, VP_LAYOUT["kdec"]:VP_LAYOUT["kdec"] + 4] = kdec
    out["vp"] = vp
    out["tabs"] = tabs
    return out


_CACHE = {}


def kernel(**inputs):
    x = np.asarray(inputs["x"], dtype=np.float32)
    positions = np.asarray(inputs["positions"]).astype(np.int32)
    B = x.shape[0]
    shared = prep_weights(inputs)
    if "nc" not in _CACHE:
        _CACHE["nc"] = build()[0]
    nc = _CACHE["nc"]
    in_maps = []
    for b in range(B):
        m = dict(shared)
        m["xT"] = np.ascontiguousarray(x[b].T)
        m["pos"] = np.ascontiguousarray(positions[b:b + 1])
        in_maps.append(m)
    res = run_bass_kernel_spmd(nc, in_maps, core_ids=list(range(B)))
    out = np.stack([np.ascontiguousarray(res.results[b]["yT"].T) for b in range(B)], 0)
    return out.astype(np.float32)
```
